# Optimizing a Trainium2 kernel written in Bass

```python
import math
import jax, jax.numpy as jnp
from jax import lax
import numpy as np

D_MODEL = 1024
BATCH = 4
SEQ = 4096
DEPTH = 1
DEC_BATCH = 8
DEC_SEQ = 2048
PAST_LEN = 128

N_META = 16
N_ATT_HEADS = 8
ATT_DH = 64
ATT_DV = 2 * ATT_DH
ATT_QK = N_ATT_HEADS * 2 * ATT_DH
ATT_V = N_ATT_HEADS * ATT_DV
Q_BLOCK = 128
NUM_BUCKETS = 32
MAX_DISTANCE = 128
SSM_HEADS = 16
SSM_HEADDIM = 64
SSM_INNER = SSM_HEADS * SSM_HEADDIM
SSM_GROUPS = 2
SSM_STATE = 64
SSM_CONV = 7
SSM_CONV_DIM = SSM_INNER + 2 * SSM_GROUPS * SSM_STATE
CHUNK = 128
D_MIX = ATT_V + SSM_INNER
D_IN_PROJ = 2 * ATT_QK + ATT_V + SSM_INNER + SSM_CONV_DIM + 2 * SSM_HEADS
D_FF = 2816
EPS = 1e-6

kernel_name = "hymba_diffattn_ssd_macaron_encoder"


def rmsnorm(x, w):
    x32 = x.astype(jnp.float32)
    y = x32 * lax.rsqrt(jnp.mean(x32 * x32, axis=-1, keepdims=True) + EPS)
    return (y * w.astype(jnp.float32)).astype(x.dtype)


def swiglu(u, wg, wu, wd):
    return (jax.nn.silu(u @ wg) * (u @ wu)) @ wd


def t5_bucket(rel):
    half = NUM_BUCKETS // 2
    max_exact = half // 2
    ret = jnp.where(rel > 0, half, 0)
    n = jnp.abs(rel)
    nf = jnp.maximum(n, 1).astype(jnp.float32)
    large = max_exact + (jnp.log(nf / max_exact) / math.log(MAX_DISTANCE / max_exact)
                         * (half - max_exact)).astype(jnp.int32)
    large = jnp.minimum(large, half - 1)
    return ret + jnp.where(n < max_exact, n, large)


def diff_attention(q, k, v, rel_bias, lq1, lk1, lq2, lk2, subln_w, layer):
    b, L, _ = q.shape
    s = L - N_META
    q = q.reshape(b, L, N_ATT_HEADS, 2, ATT_DH)
    k = k.reshape(b, L, N_ATT_HEADS, 2, ATT_DH)
    v = v.reshape(b, L, N_ATT_HEADS, ATT_DV)
    lam_init = 0.8 - 0.6 * math.exp(-0.3 * layer)
    lam = (jnp.exp(jnp.sum(lq1.astype(jnp.float32) * lk1.astype(jnp.float32)))
           - jnp.exp(jnp.sum(lq2.astype(jnp.float32) * lk2.astype(jnp.float32))) + lam_init)
    k_pos = jnp.arange(L, dtype=jnp.int32)
    scale = ATT_DH ** -0.5

    def attend(qb, q_pos):
        bias = rel_bias[t5_bucket(k_pos[None, :] - q_pos[:, None])]
        bias = jnp.transpose(bias, (2, 0, 1)).astype(jnp.float32)
        logits = jnp.einsum('bqhtd,bkhtd->bthqk', qb, k).astype(jnp.float32) * scale + bias
        p = jax.nn.softmax(logits, axis=-1)
        a = (p[:, 0] - lam * p[:, 1]).astype(v.dtype)
        return jnp.einsum('bhqk,bkhd->bqhd', a, v)

    out_meta = attend(q[:, :N_META], jnp.arange(N_META, dtype=jnp.int32))
    nblk = s // Q_BLOCK
    q_blocks = q[:, N_META:].reshape(b, nblk, Q_BLOCK, N_ATT_HEADS, 2, ATT_DH).transpose(1, 0, 2, 3, 4, 5)
    base = jnp.arange(Q_BLOCK, dtype=jnp.int32)
    out_real = lax.map(lambda a: attend(a[0], N_META + a[1] * Q_BLOCK + base),
                       (q_blocks, jnp.arange(nblk, dtype=jnp.int32)))
    out_real = out_real.transpose(1, 0, 2, 3, 4).reshape(b, s, N_ATT_HEADS, ATT_DV)
    out = jnp.concatenate([out_meta, out_real], axis=1)
    out = rmsnorm(out, subln_w) * (1.0 - lam_init)
    return out.reshape(b, L, ATT_V)


def centred_dwconv(u, w, bias):
    c = u.shape[-1]
    out = lax.conv_general_dilated(u, w[:, None, :], window_strides=(1,),
                                   padding=[(SSM_CONV // 2, SSM_CONV // 2)],
                                   dimension_numbers=('NWC', 'WIO', 'NWC'),
                                   feature_group_count=c)
    return out + bias


def ssd_chunked(x, dt, A, B, C, h0, cs):
    b, l, h, p = x.shape
    n = B.shape[-1]
    c = l // cs
    x = x.reshape(b, c, cs, h, p)
    dt = dt.reshape(b, c, cs, h)
    B = B.reshape(b, c, cs, h, n)
    C = C.reshape(b, c, cs, h, n)
    a_cum = jnp.cumsum(dt * A, axis=2)
    seg = a_cum[:, :, :, None, :] - a_cum[:, :, None, :, :]
    lower = jnp.tril(jnp.ones((cs, cs), dtype=bool))[None, None, :, :, None]
    decay = jnp.exp(jnp.where(lower, seg, -jnp.inf))
    scores = jnp.einsum('bclhn,bcshn->bclsh', C, B) * decay * dt[:, :, None, :, :]
    y_diag = jnp.einsum('bclsh,bcshp->bclhp', scores, x)
    decay_states = jnp.exp(a_cum[:, :, -1:, :] - a_cum)
    states = jnp.einsum('bclhn,bclh,bclhp->bchpn', B, decay_states * dt, x)
    chunk_decay = jnp.exp(a_cum[:, :, -1, :])

    def step(hs, inp):
        st, d = inp
        return hs * d[:, :, None, None] + st, hs

    h_last, h_prev = lax.scan(step, h0, (states.transpose(1, 0, 2, 3, 4), chunk_decay.transpose(1, 0, 2)))
    h_prev = h_prev.transpose(1, 0, 2, 3, 4)
    y_off = jnp.einsum('bclhn,bchpn,bclh->bclhp', C, h_prev, jnp.exp(a_cum))
    return (y_diag + y_off).reshape(b, l, h, p), h_last


def ssd_direction(x, dt, A, B, C, reverse):
    b, L, h, p = x.shape
    if reverse:
        x, dt, B, C = x[:, ::-1], dt[:, ::-1], B[:, ::-1], C[:, ::-1]
        first, cs1, cs2 = L - N_META, CHUNK, N_META
    else:
        first, cs1, cs2 = N_META, N_META, CHUNK
    h0 = jnp.zeros((b, h, p, B.shape[-1]), jnp.float32)
    y1, h1 = ssd_chunked(x[:, :first], dt[:, :first], A, B[:, :first], C[:, :first], h0, cs1)
    y2, _ = ssd_chunked(x[:, first:], dt[:, first:], A, B[:, first:], C[:, first:], h1, cs2)
    y = jnp.concatenate([y1, y2], axis=1)
    return y[:, ::-1] if reverse else y


def ssd_mixer(z, xbc, dt_raw, conv_w, conv_b, dt_bias_f, dt_bias_b, a_log_f, a_log_b, d_skip, norm_w):
    b, L, _ = z.shape
    xbc = jax.nn.silu(centred_dwconv(xbc, conv_w, conv_b)).astype(jnp.float32)
    xs = xbc[..., :SSM_INNER].reshape(b, L, SSM_HEADS, SSM_HEADDIM)
    hpg = SSM_HEADS // SSM_GROUPS
    Bm = jnp.repeat(xbc[..., SSM_INNER:SSM_INNER + SSM_GROUPS * SSM_STATE].reshape(b, L, SSM_GROUPS, SSM_STATE), hpg, axis=2)
    Cm = jnp.repeat(xbc[..., SSM_INNER + SSM_GROUPS * SSM_STATE:].reshape(b, L, SSM_GROUPS, SSM_STATE), hpg, axis=2)
    dt_raw = dt_raw.astype(jnp.float32)
    dt_f = jax.nn.softplus(dt_raw[..., :SSM_HEADS] + dt_bias_f.astype(jnp.float32))
    dt_b = jax.nn.softplus(dt_raw[..., SSM_HEADS:] + dt_bias_b.astype(jnp.float32))
    A_f = -jnp.exp(a_log_f.astype(jnp.float32))
    A_b = -jnp.exp(a_log_b.astype(jnp.float32))
    y = (ssd_direction(xs, dt_f, A_f, Bm, Cm, False)
         + ssd_direction(xs, dt_b, A_b, Bm, Cm, True)
         + xs * d_skip.astype(jnp.float32)[:, None])
    y = y.reshape(b, L, SSM_INNER).astype(z.dtype)
    return rmsnorm(y * jax.nn.silu(z), norm_w)


def encoder(x, meta_tokens, ffn1_norm_w, ffn1_w_gate, ffn1_w_up, ffn1_w_down, mix_norm_w, w_in,
            rel_bias, lambda_q1, lambda_k1, lambda_q2, lambda_k2, attn_subln_w, conv_w, conv_b,
            dt_bias_fwd, dt_bias_bwd, a_log_fwd, a_log_bwd, ssm_d, ssm_norm_w, w_out,
            ffn2_norm_w, ffn2_w_gate, ffn2_w_up, ffn2_w_down, final_norm_w):
    b = x.shape[0]
    meta = jnp.broadcast_to(meta_tokens[None].astype(x.dtype), (b, N_META, D_MODEL))
    h = jnp.concatenate([meta, x], axis=1)
    o1, o2, o3, o4, o5 = ATT_QK, 2 * ATT_QK, 2 * ATT_QK + ATT_V, 2 * ATT_QK + ATT_V + SSM_INNER, 2 * ATT_QK + ATT_V + SSM_INNER + SSM_CONV_DIM
    for layer in range(DEPTH):
        h = h + 0.5 * swiglu(rmsnorm(h, ffn1_norm_w[layer]), ffn1_w_gate[layer], ffn1_w_up[layer], ffn1_w_down[layer])
        u = rmsnorm(h, mix_norm_w[layer])
        proj = u @ w_in[layer]
        q, k, v, z, xbc, dt_raw = jnp.split(proj, [o1, o2, o3, o4, o5], axis=-1)
        att = diff_attention(q, k, v, rel_bias, lambda_q1[layer], lambda_k1[layer], lambda_q2[layer],
                             lambda_k2[layer], attn_subln_w[layer], layer)
        ssm = ssd_mixer(z, xbc, dt_raw, conv_w[layer], conv_b[layer], dt_bias_fwd[layer], dt_bias_bwd[layer],
                        a_log_fwd[layer], a_log_bwd[layer], ssm_d[layer], ssm_norm_w[layer])
        h = h + jnp.concatenate([att, ssm], axis=-1) @ w_out[layer]
        h = h + 0.5 * swiglu(rmsnorm(h, ffn2_norm_w[layer]), ffn2_w_gate[layer], ffn2_w_up[layer], ffn2_w_down[layer])
    h = rmsnorm(h, final_norm_w)
    return h[:, N_META:]


def setup_inputs(seed: int = 0) -> dict:
    key = jax.random.key(seed)
    ks = jax.random.split(key, 32)
    f32 = jnp.float32

    def nrm(k, shape, scale):
        return jax.random.normal(k, shape, f32) * scale

    def gain(k, shape):
        return 1.0 + 0.02 * jax.random.normal(k, shape, f32)

    def dt_bias(k):
        u = jax.random.uniform(k, (DEPTH, SSM_HEADS), f32)
        dt = jnp.exp(u * (math.log(0.1) - math.log(0.001)) + math.log(0.001))
        return dt + jnp.log(-jnp.expm1(-dt))

    def a_log(k):
        return jnp.log(jax.random.uniform(k, (DEPTH, SSM_HEADS), f32, 1.0, 16.0))

    return {
        "x_prompt": nrm(ks[0], (BATCH, SEQ, D_MODEL), 1.0),
        "x_sample": nrm(ks[1], (DEC_BATCH, DEC_SEQ, D_MODEL), 1.0),
        "meta_tokens": nrm(ks[2], (N_META, D_MODEL), 1.0),
        "ffn1_norm_w": gain(ks[3], (DEPTH, D_MODEL)),
        "ffn1_w_gate": nrm(ks[4], (DEPTH, D_MODEL, D_FF), D_MODEL ** -0.5),
        "ffn1_w_up": nrm(ks[5], (DEPTH, D_MODEL, D_FF), D_MODEL ** -0.5),
        "ffn1_w_down": nrm(ks[6], (DEPTH, D_FF, D_MODEL), D_FF ** -0.5),
        "mix_norm_w": gain(ks[7], (DEPTH, D_MODEL)),
        "w_in": nrm(ks[8], (DEPTH, D_MODEL, D_IN_PROJ), D_MODEL ** -0.5),
        "rel_bias": nrm(ks[9], (NUM_BUCKETS, N_ATT_HEADS), 0.5),
        "lambda_q1": nrm(ks[10], (DEPTH, ATT_DH), 0.1),
        "lambda_k1": nrm(ks[11], (DEPTH, ATT_DH), 0.1),
        "lambda_q2": nrm(ks[12], (DEPTH, ATT_DH), 0.1),
        "lambda_k2": nrm(ks[13], (DEPTH, ATT_DH), 0.1),
        "attn_subln_w": gain(ks[14], (DEPTH, ATT_DV)),
        "conv_w": nrm(ks[15], (DEPTH, SSM_CONV, SSM_CONV_DIM), SSM_CONV ** -0.5),
        "conv_b": nrm(ks[16], (DEPTH, SSM_CONV_DIM), 0.02),
        "dt_bias_fwd": dt_bias(ks[17]),
        "dt_bias_bwd": dt_bias(ks[18]),
        "a_log_fwd": a_log(ks[19]),
        "a_log_bwd": a_log(ks[20]),
        "ssm_d": gain(ks[21], (DEPTH, SSM_HEADS)),
        "ssm_norm_w": gain(ks[22], (DEPTH, SSM_INNER)),
        "w_out": nrm(ks[23], (DEPTH, D_MIX, D_MODEL), D_MIX ** -0.5),
        "ffn2_norm_w": gain(ks[24], (DEPTH, D_MODEL)),
        "ffn2_w_gate": nrm(ks[25], (DEPTH, D_MODEL, D_FF), D_MODEL ** -0.5),
        "ffn2_w_up": nrm(ks[26], (DEPTH, D_MODEL, D_FF), D_MODEL ** -0.5),
        "ffn2_w_down": nrm(ks[27], (DEPTH, D_FF, D_MODEL), D_FF ** -0.5),
        "final_norm_w": gain(ks[28], (D_MODEL,)),
    }


def reference(x_prompt, x_sample, meta_tokens, ffn1_norm_w, ffn1_w_gate, ffn1_w_up, ffn1_w_down, mix_norm_w,
              w_in, rel_bias, lambda_q1, lambda_k1, lambda_q2, lambda_k2, attn_subln_w, conv_w, conv_b,
              dt_bias_fwd, dt_bias_bwd, a_log_fwd, a_log_bwd, ssm_d, ssm_norm_w, w_out,
              ffn2_norm_w, ffn2_w_gate, ffn2_w_up, ffn2_w_down, final_norm_w):
    weights = (meta_tokens, ffn1_norm_w, ffn1_w_gate, ffn1_w_up, ffn1_w_down, mix_norm_w, w_in,
               rel_bias, lambda_q1, lambda_k1, lambda_q2, lambda_k2, attn_subln_w, conv_w, conv_b,
               dt_bias_fwd, dt_bias_bwd, a_log_fwd, a_log_bwd, ssm_d, ssm_norm_w, w_out,
               ffn2_norm_w, ffn2_w_gate, ffn2_w_up, ffn2_w_down, final_norm_w)
    y_prompt = encoder(x_prompt, *weights)
    y_sample = encoder(x_sample, *weights)
    return (y_prompt, y_sample)
```

```python
import math
from contextlib import ExitStack
import numpy as np
import concourse.bass as bass
import concourse.mybir as mybir
from concourse.bass_utils import run_bass_kernel_spmd

F32 = mybir.dt.float32
BF16 = mybir.dt.bfloat16
AF = mybir.ActivationFunctionType
ALU = mybir.AluOpType

ENGS = ("pe", "act", "dve", "pool", "sp")
NMETA = 16
LS = 2064
NCORES = 8
import os
DEBUG = os.environ.get('KDEBUG', '')
NOBIAS = bool(int(os.environ.get('NOBIAS', '0')))
EPI_DELAY = int(os.environ.get('EPI_DELAY', '1'))
START_ALL = bool(int(os.environ.get('START_ALL', '0')))
LP = 4112
LO = 2064
EPS = 1e-6
ZW = 1536


class Dep:
    __slots__ = ("w", "r")

    def __init__(self):
        self.w = None
        self.r = []


class DSem:
    __slots__ = ("h", "count", "name")

    def __init__(self, name):
        self.h = None
        self.count = 0
        self.name = name


class Op:
    __slots__ = ("eng", "fn", "waits", "need_inc", "dsem", "dcount", "idx", "cnt")

    def __init__(self, eng, fn):
        self.eng = eng
        self.fn = fn
        self.waits = []
        self.need_inc = False
        self.dsem = None
        self.dcount = 0
        self.cnt = 0


class Prog:
    def __init__(self):
        self.ops = {e: [] for e in ENGS}
        self.dsems = []
        self.alldeps = []

    def dep(self):
        d = Dep()
        self.alldeps.append(d)
        return d

    def deps(self, n):
        return [self.dep() for _ in range(n)]

    def dsem(self, name):
        d = DSem(name)
        self.dsems.append(d)
        return d

    def _add(self, op, reads, writes):
        ws = []
        for d in reads:
            if d.w is not None:
                ws.append(d.w)
        for d in writes:
            if d.w is not None:
                ws.append(d.w)
            ws.extend(d.r)
        seen = set()
        for w in ws:
            if id(w) in seen or w is op:
                continue
            seen.add(id(w))
            op.waits.append(w)
            if w.dsem is None:
                w.need_inc = True
        for d in reads:
            d.r.append(op)
        for d in writes:
            d.w = op
            d.r = []
        op.idx = len(self.ops[op.eng])
        self.ops[op.eng].append(op)
        return op

    def op(self, eng, fn, reads=(), writes=()):
        return self._add(Op(eng, fn), reads, writes)

    def dma(self, queue, fn, dsem, reads=(), writes=()):
        op = Op(queue, fn)
        op.dsem = dsem
        dsem.count += 16
        op.dcount = dsem.count
        return self._add(op, reads, writes)

    def barrier(self):
        deps = list(self.alldeps)
        for e in ENGS:
            self.op(e, None, writes=deps)

    def emit(self, nc, stack):
        sems = {e: stack.enter_context(nc.semaphore("s_" + e)) for e in ENGS}
        for i, d in enumerate(self.dsems):
            d.h = stack.enter_context(nc.semaphore("d%d_%s" % (i, d.name)))
        for e in ENGS:
            c = 0
            for op in self.ops[e]:
                if op.dsem is None and op.need_inc:
                    c += 1
                op.cnt = c
        block = stack.enter_context(nc.Block())
        prog = self

        def run(e, eng):
            waited = {}
            for op in prog.ops[e]:
                for w in op.waits:
                    if w.dsem is not None:
                        key, val, h = ("d", id(w.dsem)), w.dcount, w.dsem.h
                    else:
                        key, val, h = ("e", w.eng), w.cnt, sems[w.eng]
                    if waited.get(key, 0) >= val:
                        continue
                    waited[key] = val
                    eng.wait_ge(h, val)
                if op.fn is None:
                    if op.need_inc:
                        eng.nop().then_inc(sems[e], 1)
                    continue
                ins = op.fn(eng)
                if op.dsem is not None:
                    ins.then_inc(op.dsem.h, 16)
                elif op.need_inc:
                    ins.then_inc(sems[e], 1)

        @block.tensor
        def _(eng):
            run("pe", eng)

        @block.scalar
        def _(eng):
            run("act", eng)

        @block.vector
        def _(eng):
            run("dve", eng)

        @block.gpsimd
        def _(eng):
            run("pool", eng)

        @block.sync
        def _(eng):
            run("sp", eng)


class Buf:
    __slots__ = ("ap", "dep", "dsem")

    def __init__(self, ap, dep, dsem=None):
        self.ap = ap
        self.dep = dep
        self.dsem = dsem


class Ring:
    def __init__(self, bufs):
        self.bufs = bufs
        self.i = 0

    def next(self):
        b = self.bufs[self.i % len(self.bufs)]
        self.i += 1
        return b


class Bump:
    def __init__(self, arena, regions):
        self.arena = arena
        self.regions = [[s, s + n] for s, n in regions]

    def alloc(self, free_shape, dt):
        esz = 2 if dt == BF16 else 4
        n = 1
        for s in free_shape:
            n *= s
        nbytes = (n * esz + 31) // 32 * 32
        for r in self.regions:
            if r[1] - r[0] >= nbytes:
                off = r[0]
                r[0] += nbytes
                ap = self.arena[:, off // 4:(off + nbytes) // 4]
                if dt == BF16:
                    ap = ap.bitcast(BF16)
                ap = ap[:, 0:n]
                if len(free_shape) == 2:
                    ap = ap.rearrange("p (a b) -> p a b", a=free_shape[0])
                elif len(free_shape) == 3:
                    ap = ap.rearrange("p (a b c) -> p a b c", a=free_shape[0], b=free_shape[1])
                return ap
        raise RuntimeError("arena region overflow: need %d bytes; regions %s" % (nbytes, self.regions))


def tiles_of(L, w=512):
    out = []
    t = 0
    while t < L:
        out.append((t, min(w, L - t)))
        t += w
    return out


def t5_bucket_np(rel):
    rel = np.asarray(rel, np.int64)
    half, max_exact = 16, 8
    ret = np.where(rel > 0, half, 0)
    n = np.abs(rel)
    nf = np.maximum(n, 1).astype(np.float32)
    large = max_exact + (np.log(nf / np.float32(max_exact)) / np.float32(math.log(128 / max_exact))
                         * np.float32(half - max_exact)).astype(np.int32)
    large = np.minimum(large, half - 1)
    return ret + np.where(n < max_exact, n, large)


W_SPECS = [
    ("wg1", 11, 2048), ("wu1", 11, 2048), ("wd1", 8, 2816),
    ("watt", 8, 3072), ("wzx", 8, 2048), ("wbc", 1, 2048), ("wdtS", 1, 256), ("wdtP", 1, 256),
    ("wo", 8, 2048),
    ("wg2", 11, 2048), ("wu2", 11, 2048), ("wd2", 8, 2816),
]

R0, R1, R2, R3 = 0, 33024, 66048, 131840
R3SZ = 70912
CN0 = R3 + R3SZ
CNSZ = 8192
ARENA_BYTES = CN0 + CNSZ


def build_program(do_S=True, do_P=True, do_attn=True, do_ssm=True):
    nc = bass.Bass("TRN2", target_bir_lowering=False)
    P = Prog()
    din = {}

    def dram_in(name, shape):
        din[name] = nc.dram_tensor(name, list(shape), F32, kind="ExternalInput").ap()
        return din[name]

    xs = dram_in("xs", [1024, LS])
    xp = dram_in("xp", [1024, LP])
    wsrc = {}
    wbf = {}
    wdep = {}
    for name, ns, el in W_SPECS:
        wsrc[name] = dram_in(name, [ns, 128, el])
        wbf[name] = nc.dram_tensor(name + "_bf", [ns, 128, el], BF16).ap()
        wdep[name] = P.deps(ns)
    nrm_d = dram_in("nrm", [128, 40])
    vec_d = {"S": dram_in("vecS", [128, 88]), "P": dram_in("vecP", [128, 88])}
    row_d = {"S": dram_in("rowS", [1, 192]), "P": dram_in("rowP", [1, 192])}
    lam_d = dram_in("lam", [1, 256])
    relb_d = {"S": dram_in("relbS", [32, 8]), "P": dram_in("relbP", [32, 8])}
    oh_d = dram_in("oh", [32, ZW])
    ys = nc.dram_tensor("ys", [1024, LO], F32, kind="ExternalOutput").ap()
    yp = nc.dram_tensor("yp", [1024, LO], F32, kind="ExternalOutput").ap()
    h1s = nc.dram_tensor("h1scr", [1024, LO], F32).ap()
    dbg_d = nc.dram_tensor("dbg", [128, 16 * LO], BF16, kind="ExternalOutput").ap() if DEBUG else None
    dbgu_d = nc.dram_tensor("dbgu", [128, 8 * (LS if DEBUG == 'S' else LP)], BF16, kind="ExternalOutput").ap() if DEBUG else None
    zscr = {"S": nc.dram_tensor("zscrS", [8, 128, ZW], F32).ap(), "P": nc.dram_tensor("zscrP", [8, 128, ZW], F32).ap()}
    zdep = {"S": P.deps(8), "P": P.deps(8)}

    with ExitStack() as st:
        arena_t = st.enter_context(nc.sbuf_tensor("arena", [128, ARENA_BYTES // 4], F32))
        arena = arena_t[:, :]
        psum_t = st.enter_context(nc.psum_tensor("psum", [128, 4096], F32))
        banks = [Buf(psum_t[:, b * 512:(b + 1) * 512], P.dep()) for b in range(8)]

        for name, ns, el in W_SPECS:
            csem_w = P.dsem("c_" + name)
            for s_ in range(ns):
                P.dma("pool", (lambda e, o=wbf[name][s_], i=wsrc[name][s_]: e.dma_start(out=o, in_=i)), csem_w, writes=[wdep[name][s_]])
            for d in wdep[name]:
                d.w.dcount = csem_w.count

        cb = Bump(arena, [(CN0, CNSZ)])
        ident = cb.alloc([128], BF16)
        onesb = cb.alloc([128], BF16)
        onesf = cb.alloc([128], F32)
        tri2 = cb.alloc([2, 128], F32)
        trif = tri2[:, 0, :]
        trib = tri2[:, 1, :]
        nrm = cb.alloc([40], F32)
        epsb = cb.alloc([1], F32)
        neglam = cb.alloc([1], F32)
        lamt = cb.alloc([256], F32)
        lam2 = cb.alloc([8], F32)
        subw = cb.alloc([128], F32)
        vec = {k: cb.alloc([88], F32) for k in "SP"}
        row = {k: cb.alloc([192], F32) for k in "SP"}
        negA = {k: cb.alloc([32], F32) for k in "SP"}
        cdep = P.dep()
        csem = P.dsem("const")
        tmpf = cb.alloc([128], F32)

        P.op("pool", lambda e: e.memset(onesf, 1.0), writes=[cdep])
        P.op("pool", lambda e: e.memset(epsb, EPS), writes=[cdep])
        P.op("pool", lambda e: e.memset(tmpf, 0.0), writes=[cdep])
        P.op("pool", lambda e: e.affine_select(out=tmpf, in_=tmpf, compare_op=ALU.not_equal, fill=1.0, base=0,
                                               pattern=[[-1, 128]], channel_multiplier=1), reads=[cdep], writes=[cdep])
        P.op("dve", lambda e: e.tensor_copy(ident, tmpf), reads=[cdep], writes=[cdep])
        P.op("dve", lambda e: e.tensor_copy(onesb, onesf), reads=[cdep], writes=[cdep])
        P.op("pool", lambda e: e.affine_select(out=trif, in_=onesf, compare_op=ALU.is_ge, fill=0.0, base=0,
                                               pattern=[[1, 128]], channel_multiplier=-1), reads=[cdep], writes=[cdep])
        P.op("pool", lambda e: e.affine_select(out=trib, in_=onesf, compare_op=ALU.is_ge, fill=0.0, base=0,
                                               pattern=[[-1, 128]], channel_multiplier=1), reads=[cdep], writes=[cdep])
        P.dma("sp", lambda e: e.dma_start(out=nrm, in_=nrm_d), csem, writes=[cdep])
        for k in "SP":
            P.dma("sp", lambda e, k=k: e.dma_start(out=vec[k], in_=vec_d[k]), csem, writes=[cdep])
            P.dma("sp", lambda e, k=k: e.dma_start(out=row[k], in_=row_d[k].partition_broadcast(128)), csem, writes=[cdep])
        P.dma("sp", lambda e: e.dma_start(out=lamt, in_=lam_d.partition_broadcast(128)), csem, writes=[cdep])
        P.op("dve", lambda e: e.tensor_tensor(lamt[:, 0:64], lamt[:, 0:64], lamt[:, 64:128], ALU.mult), reads=[cdep], writes=[cdep])
        P.op("dve", lambda e: e.tensor_tensor(lamt[:, 128:192], lamt[:, 128:192], lamt[:, 192:256], ALU.mult), reads=[cdep], writes=[cdep])
        P.op("dve", lambda e: e.reduce_sum(lam2[:, 0:1], lamt[:, 0:64], axis=mybir.AxisListType.X), reads=[cdep], writes=[cdep])
        P.op("dve", lambda e: e.reduce_sum(lam2[:, 1:2], lamt[:, 128:192], axis=mybir.AxisListType.X), reads=[cdep], writes=[cdep])
        P.op("act", lambda e: e.activation(lam2[:, 2:4], lam2[:, 0:2], AF.Exp), reads=[cdep], writes=[cdep])
        P.op("dve", lambda e: e.scalar_tensor_tensor(neglam, lam2[:, 3:4], -0.2, lam2[:, 2:3], ALU.add, ALU.subtract), reads=[cdep], writes=[cdep])
        P.op("dve", lambda e: e.tensor_scalar(subw, row["S"][:, 64:192], 0.8, None, ALU.mult), reads=[cdep], writes=[cdep])
        for k in "SP":
            P.op("act", lambda e, k=k: e.activation(negA[k], row[k][:, 32:64], AF.Exp), reads=[cdep], writes=[cdep])
            P.op("dve", lambda e, k=k: e.tensor_scalar(negA[k], negA[k], -1.0, None, ALU.mult), reads=[cdep], writes=[cdep])

        def build_bias(kind):
            wb_ = Bump(arena, [(R3, R3SZ)])
            oh = wb_.alloc([ZW], F32)
            relb = wb_.alloc([8], F32)
            relbb = wb_.alloc([8, 128], F32)
            zsb = [wb_.alloc([ZW], F32) for _ in range(2)]
            zd = P.deps(2)
            d0 = P.dep()
            s0 = P.dsem("bias" + kind)
            s1 = P.dsem("biasst" + kind)
            P.dma("sp", lambda e: e.dma_start(out=oh[0:32, :], in_=oh_d), s0, writes=[d0])
            P.dma("sp", lambda e: e.dma_start(out=relb[0:32, :], in_=relb_d[kind]), s0, writes=[d0])
            for h in range(8):
                P.op("dve", lambda e, h=h: e.tensor_copy(relbb[0:32, h, :], relb[0:32, h:h + 1].to_broadcast([32, 128])),
                     reads=[d0], writes=[d0])
            for h in range(8):
                for j in range(3):
                    bk = banks[j]
                    P.op("pe", lambda e, h=h, j=j, bk=bk: e.matmul(bk.ap, relbb[0:32, h, :], oh[0:32, j * 512:(j + 1) * 512],
                                                                 start=True, stop=True), reads=[d0], writes=[bk.dep])
                    P.op("act", lambda e, h=h, j=j, bk=bk: e.copy(zsb[h % 2][:, j * 512:(j + 1) * 512], bk.ap),
                         reads=[bk.dep], writes=[zd[h % 2]])
                P.dma("pool", lambda e, h=h: e.dma_start(out=zscr[kind][h], in_=zsb[h % 2]), s1,
                      reads=[zd[h % 2]], writes=[zdep[kind][h]])

        pool_i = [0]

        def bank(pool):
            b = banks[pool[pool_i[0] % len(pool)]]
            pool_i[0] += 1
            return b

        def load_w(ring, name, s):
            b = ring.next()
            n = 1
            for x in b.ap.shape[1:]:
                n *= x
            src = wbf[name][s]
            dst = b.ap
            if len(b.ap.shape) == 3:
                src = src.rearrange("p (a b) -> p a b", a=b.ap.shape[1])
            P.dma("sp", lambda e: e.dma_start(out=dst, in_=src), b.dsem, reads=[wdep[name][s]], writes=[b.dep])
            return b

        def norm_rstd(src, src_deps, T, sq, sq_dep, rstd, rstd_dep, pool):
            P.op("act", lambda e: e.activation(sq[:, :, :T], src[:, :, :T], AF.Square), reads=src_deps, writes=[sq_dep])
            bk = bank(pool)

            def mm(e):
                for c in range(8):
                    ins = e.matmul(bk.ap[:, :T], onesb, sq[:, c, :T], start=(c == 0), stop=(c == 7))
                return ins
            P.op("pe", mm, reads=[sq_dep, cdep], writes=[bk.dep])
            P.op("act", lambda e: e.activation(rstd[:, :T], bk.ap[:, :T], AF.Sqrt, bias=epsb, scale=1.0 / 1024.0),
                 reads=[bk.dep, cdep], writes=[rstd_dep])
            P.op("dve", lambda e: e.reciprocal(rstd[:, :T], rstd[:, :T]), reads=[rstd_dep], writes=[rstd_dep])

        class FfnCtx:
            pass

        def alloc_ffn(bump, with_wo):
            c = FfnCtx()
            c.hT = Ring([Buf(bump.alloc([8, 512], F32), P.deps(8), P.dsem("hT%d" % i)) for i in range(2)])
            c.u = bump.alloc([8, 512], BF16)
            c.u_dep = P.deps(8)
            c.sq = bump.alloc([8, 512], BF16)
            c.sq_dep = P.dep()
            c.act = bump.alloc([22, 512], BF16)
            c.act_dep = P.deps(22)
            c.sg = Ring([Buf(bump.alloc([512], F32), P.dep()) for _ in range(2)])
            c.rstd = bump.alloc([512], F32)
            c.rstd_dep = P.dep()
            c.wg = Ring([Buf(bump.alloc([8, 256], BF16), P.dep(), P.dsem("wg%d" % i)) for i in range(2)])
            c.wu = Ring([Buf(bump.alloc([8, 256], BF16), P.dep(), P.dsem("wu%d" % i)) for i in range(2)])
            c.wd = Ring([Buf(bump.alloc([22, 128], BF16), P.dep(), P.dsem("wd%d" % i)) for i in range(2)])
            if with_wo:
                c.wo = Ring([Buf(bump.alloc([16, 128], BF16), P.dep(), P.dsem("wo%d" % i)) for i in range(2)])
            c.pool = [0, 1, 2, 3, 4, 5, 6, 7]
            return c

        def ffn(c, hb, T, nk, sfx):
            hT = hb.ap
            norm_rstd(hT, hb.dep, T, c.sq, c.sq_dep, c.rstd, c.rstd_dep, c.pool)
            for ch in range(8):
                P.op("dve", lambda e, ch=ch: e.scalar_tensor_tensor(c.u[:, ch, :T], hT[:, ch, :T], nrm[:, nk * 8 + ch:nk * 8 + ch + 1],
                                                                    c.rstd[:, :T], ALU.mult, ALU.mult),
                     reads=[hb.dep[ch], c.rstd_dep, cdep], writes=[c.u_dep[ch]])
            wgb = load_w(c.wg, "wg" + sfx, 0)
            wub = load_w(c.wu, "wu" + sfx, 0)
            for g in range(11):
                nxt = None
                if g + 1 < 11:
                    nxt = (load_w(c.wg, "wg" + sfx, g + 1), load_w(c.wu, "wu" + sfx, g + 1))
                for k in range(2):
                    f = 2 * g + k
                    pg = bank(c.pool)

                    def mmg(e, wgb=wgb, k=k, pg=pg):
                        for ch in range(8):
                            ins = e.matmul(pg.ap[:, :T], wgb.ap[:, ch, k * 128:(k + 1) * 128], c.u[:, ch, :T], start=(ch == 0), stop=(ch == 7))
                        return ins
                    P.op("pe", mmg, reads=[wgb.dep] + c.u_dep, writes=[pg.dep])
                    pu = bank(c.pool)

                    def mmu(e, wub=wub, k=k, pu=pu):
                        for ch in range(8):
                            ins = e.matmul(pu.ap[:, :T], wub.ap[:, ch, k * 128:(k + 1) * 128], c.u[:, ch, :T], start=(ch == 0), stop=(ch == 7))
                        return ins
                    P.op("pe", mmu, reads=[wub.dep] + c.u_dep, writes=[pu.dep])
                    sg = c.sg.next()
                    P.op("act", lambda e, sg=sg, pg=pg: e.activation(sg.ap[:, :T], pg.ap[:, :T], AF.Silu), reads=[pg.dep], writes=[sg.dep])
                    P.op("dve", lambda e, sg=sg, pu=pu, f=f: e.tensor_tensor(c.act[:, f, :T], sg.ap[:, :T], pu.ap[:, :T], ALU.mult),
                         reads=[sg.dep, pu.dep], writes=[c.act_dep[f]])
                if nxt is not None:
                    wgb, wub = nxt
            wdb = load_w(c.wd, "wd" + sfx, 0)
            for ch in range(8):
                nxt = load_w(c.wd, "wd" + sfx, ch + 1) if ch + 1 < 8 else None
                pd = bank(c.pool)

                def mmd(e, wdb=wdb, pd=pd):
                    for f in range(22):
                        ins = e.matmul(pd.ap[:, :T], wdb.ap[:, f, :], c.act[:, f, :T], start=(f == 0), stop=(f == 21))
                    return ins
                P.op("pe", mmd, reads=[wdb.dep] + c.act_dep, writes=[pd.dep])
                P.op("dve", lambda e, ch=ch, pd=pd: e.scalar_tensor_tensor(hT[:, ch, :T], pd.ap[:, :T], 0.5, hT[:, ch, :T], ALU.mult, ALU.add),
                     reads=[pd.dep, hb.dep[ch]], writes=[hb.dep[ch]])
                if nxt is not None:
                    wdb = nxt

        def run_seq(kind, x_d, L, y_d):
            U = Bump(arena, [(R2, 65792)]).alloc([8, L], BF16)
            U_tdeps = {}
            tl = tiles_of(L)
            for (t0, T) in tl:
                U_tdeps[t0] = P.dep()
            MIX = Bump(arena, [(R0, 66048)]).alloc([16, LO], BF16)
            mix_dep = [P.dep() for _ in range(16)]
            x_v = x_d.rearrange("(c p) t -> p c t", p=128)
            h1_v = h1s.rearrange("(c p) t -> p c t", p=128)
            y_v = y_d.rearrange("(c p) t -> p c t", p=128)
            h1_dep = {t0: P.dep() for (t0, T) in tiles_of(LO)}
            st_sem = P.dsem("store" + kind)

            cA = alloc_ffn(Bump(arena, [(R0, 66048), (R3, R3SZ)]), False)

            def load_x(t0, T):
                hb = cA.hT.next()
                P.dma("sp", lambda e: e.dma_start(out=hb.ap[:, :, :T], in_=x_v[:, :, t0:t0 + T]), hb.dsem, writes=hb.dep)
                return hb
            hb = load_x(*tl[0])
            for i, (t0, T) in enumerate(tl):
                nxt = load_x(*tl[i + 1]) if i + 1 < len(tl) else None
                ffn(cA, hb, T, 0, "1")
                if t0 < LO:
                    T2 = min(T, LO - t0)
                    P.dma("pool", lambda e, hb=hb, t0=t0, T2=T2: e.dma_start(out=h1_v[:, :, t0:t0 + T2], in_=hb.ap[:, :, :T2]), st_sem,
                          reads=hb.dep, writes=[h1_dep[t0]])
                norm_rstd(hb.ap, hb.dep, T, cA.sq, cA.sq_dep, cA.rstd, cA.rstd_dep, cA.pool)
                for ch in range(8):
                    P.op("dve", lambda e, ch=ch, hb=hb, t0=t0, T=T: e.scalar_tensor_tensor(U[:, ch, t0:t0 + T], hb.ap[:, ch, :T], nrm[:, 8 + ch:9 + ch],
                                                                                             cA.rstd[:, :T], ALU.mult, ALU.mult),
                         reads=[hb.dep[ch], cA.rstd_dep, cdep], writes=[U_tdeps[t0]])
                hb = nxt
            P.barrier()
            U_all = list(U_tdeps.values())

            if do_ssm:
                ssm_phase(kind, L, U, U_all, MIX, mix_dep)
                P.barrier()
            else:
                P.op("pool", lambda e: e.memset(MIX[:, 8:16, :], 0.0), writes=mix_dep[8:16])
            if do_attn:
                build_bias(kind)
                P.barrier()
                attn_phase(kind, L, U, U_all, MIX, mix_dep)
                P.barrier()
            else:
                P.op("pool", lambda e: e.memset(MIX[:, 0:8, :], 0.0), writes=mix_dep[0:8])
                P.barrier()

            if DEBUG and kind == DEBUG:
                P.dma("pool", lambda e: e.dma_start(out=dbgu_d.rearrange("p (a b) -> p a b", a=8), in_=U), st_sem, reads=U_all, writes=[P.dep()])
                P.dma("pool", lambda e: e.dma_start(out=dbg_d.rearrange("p (a b) -> p a b", a=16), in_=MIX), st_sem, reads=mix_dep, writes=[P.dep()])
            if DEBUG:
                P.barrier()
            c = alloc_ffn(Bump(arena, [(R2, 65792 + R3SZ)]), True)
            tlo = tiles_of(LO)

            def load_h(t0, T):
                hb = c.hT.next()
                P.dma("sp", lambda e: e.dma_start(out=hb.ap[:, :, :T], in_=h1_v[:, :, t0:t0 + T]), hb.dsem, reads=[h1_dep[t0]], writes=hb.dep)
                return hb
            hb = load_h(*tlo[0])
            for i, (t0, T) in enumerate(tlo):
                nxt = load_h(*tlo[i + 1]) if i + 1 < len(tlo) else None
                norm_rstd(MIX[:, 8:16, t0:t0 + T], mix_dep[8:16], T, c.sq, c.sq_dep, c.rstd, c.rstd_dep, c.pool)
                for j in range(8):
                    P.op("dve", lambda e, j=j, t0=t0, T=T: e.scalar_tensor_tensor(c.u[:, j, :T], MIX[:, 8 + j, t0:t0 + T], nrm[:, 32 + j:33 + j],
                                                                                    c.rstd[:, :T], ALU.mult, ALU.mult),
                         reads=[mix_dep[8 + j], c.rstd_dep, cdep], writes=[c.u_dep[j]])
                wob = load_w(c.wo, "wo", 0)
                for ch in range(8):
                    nw = load_w(c.wo, "wo", ch + 1) if ch + 1 < 8 else None
                    po = bank(c.pool)

                    def mmo(e, wob=wob, po=po, t0=t0, T=T):
                        for k in range(8):
                            e.matmul(po.ap[:, :T], wob.ap[:, k, :], MIX[:, k, t0:t0 + T], start=(k == 0), stop=False)
                        for j in range(8):
                            ins = e.matmul(po.ap[:, :T], wob.ap[:, 8 + j, :], c.u[:, j, :T], start=False, stop=(j == 7))
                        return ins
                    P.op("pe", mmo, reads=[wob.dep] + mix_dep[0:8] + c.u_dep, writes=[po.dep])
                    P.op("dve", lambda e, ch=ch, po=po, hb=hb, T=T: e.tensor_tensor(hb.ap[:, ch, :T], hb.ap[:, ch, :T], po.ap[:, :T], ALU.add),
                         reads=[po.dep, hb.dep[ch]], writes=[hb.dep[ch]])
                    if nw is not None:
                        wob = nw
                ffn(c, hb, T, 2, "2")
                norm_rstd(hb.ap, hb.dep, T, c.sq, c.sq_dep, c.rstd, c.rstd_dep, c.pool)
                for ch in range(8):
                    P.op("dve", lambda e, ch=ch, hb=hb, T=T: e.scalar_tensor_tensor(hb.ap[:, ch, :T], hb.ap[:, ch, :T], nrm[:, 24 + ch:25 + ch],
                                                                                      c.rstd[:, :T], ALU.mult, ALU.mult),
                         reads=[hb.dep[ch], c.rstd_dep, cdep], writes=[hb.dep[ch]])
                P.dma("pool", lambda e, hb=hb, t0=t0, T=T: e.dma_start(out=y_v[:, :, t0:t0 + T], in_=hb.ap[:, :, :T]), st_sem,
                      reads=hb.dep, writes=[out_dep])
                hb = nxt
            P.barrier()

        def attn_phase(kind, L, U, U_all, MIX, mix_dep):
            wk = Bump(arena, [(R3, R3SZ)])
            qz = wk.alloc([2, LO], BF16)
            kT = wk.alloc([L], BF16)
            blks = tiles_of(L, 128)
            nb = len(blks)
            V1 = wk.alloc([nb, 132], BF16)
            qd, kd, vd = P.dep(), P.dep(), P.dep()
            ering = Ring([Buf(wk.alloc([512], BF16), P.dep()) for _ in range(6)])
            tring = Ring([Buf(wk.alloc([512], F32), P.dep()) for _ in range(2)])
            Oc = wk.alloc([8, 129], F32)
            ocd = P.dep()
            wring = Ring([Buf(wk.alloc([8, 384], BF16), P.dep(), P.dsem("watt%d" % i)) for i in range(2)])
            wdt_ring = Ring([Buf(wk.alloc([1152], F32), P.dep(), P.dsem("wdtile%d" % i)) for i in range(2)])
            rec = wk.alloc([8, 2], F32)
            t1 = Ring([Buf(wk.alloc([128], F32), P.dep()) for _ in range(2)])
            o2 = Ring([Buf(wk.alloc([128], F32), P.dep()) for _ in range(8)])
            junk = wk.alloc([128], F32)
            ssq_all = wk.alloc([8, 4], F32)
            epi_n = [0]
            attb = Ring([Buf(wk.alloc([128], BF16), P.dep()) for _ in range(8)])
            ssqd = P.dep()
            spool = [0, 1, 6, 7]
            tbank = banks[2]
            obanks = [banks[3], banks[4], banks[5]]
            tp_bf = tbank.ap[:, 0:256].bitcast(BF16)
            P.op("dve", lambda e: e.memset(V1[:, :, 128:130], 1.0), writes=[vd])
            P.op("pool", lambda e: e.memset(qz, 0.0), writes=[qd])
            qtiles = tiles_of(LO)
            ktiles = tiles_of(L)
            rel_lo, rel_hi = -(L + 8), L + 8
            btab = t5_bucket_np(np.arange(rel_lo, rel_hi + 1))

            def classify(k0, KW, q0, QW):
                lo = k0 - (q0 + QW - 1)
                hi = (k0 + KW - 1) - q0
                bs = btab[lo - rel_lo:hi - rel_lo + 1]
                if (bs == 31).all():
                    return "pos"
                if (bs == 15).all():
                    return "neg"
                return "near"

            LA = int(os.environ.get("ATT_LA", "3"))
            gstate = {"gi": 0}
            pending = []

            def run_pending(force=False):
                keep = []
                for due, fn in pending:
                    if force or due <= gstate["gi"]:
                        fn()
                    else:
                        keep.append((due, fn))
                pending[:] = keep

            def defer(delay, fn):
                pending.append((gstate["gi"] + delay, fn))

            def attend_head(h, wdb):
                units = []
                qinfo = {}
                for qi, (q0, QW) in enumerate(qtiles):
                    nqb = (QW + 127) // 128
                    oslot = []
                    for t in range(2):
                        for j in range(nqb):
                            idx = t * nqb + j
                            oslot.append((obanks[idx // 3], (idx % 3) * 129, idx % 3 == 0))
                    qinfo[qi] = (q0, QW, nqb, oslot)
                    for bi, (k0, KW) in enumerate(blks):
                        for t in range(2):
                            units.append((qi, bi, k0, KW, t))
                odeps = [b.dep for b in obanks]

                def front(u):
                    qi, bi, k0, KW, t = u
                    q0, QW, nqb, oslot = qinfo[qi]
                    kindb = classify(k0, KW, q0, QW)
                    d = k0 - q0
                    sbk = bank(spool)
                    P.op("pe", lambda e: e.matmul(sbk.ap[:KW, :QW], kT[:, k0:k0 + KW], qz[:, t, q0:q0 + QW],
                                                  start=True, stop=True), reads=[qd, kd], writes=[sbk.dep])
                    eb = ering.next()
                    if kindb == "near":
                        assert 0 <= 512 - d and 512 - d + QW <= 1152
                        tb = tring.next()
                        P.op("dve", lambda e: e.scalar_tensor_tensor(tb.ap[:KW, :QW], sbk.ap[:KW, :QW], 0.125, wdb.ap[:KW, 512 - d:512 - d + QW],
                                                                     ALU.mult, ALU.add), reads=[sbk.dep, wdb.dep], writes=[tb.dep])
                        P.op("act", lambda e: e.activation(eb.ap[:KW, :QW], tb.ap[:KW, :QW], AF.Exp), reads=[tb.dep], writes=[eb.dep])
                    else:
                        col = 0 if kindb == "pos" else 1151
                        P.op("act", lambda e: e.activation(eb.ap[:KW, :QW], sbk.ap[:KW, :QW], AF.Exp, bias=wdb.ap[:KW, col:col + 1], scale=0.125),
                             reads=[sbk.dep, wdb.dep], writes=[eb.dep])
                    return eb

                def back(u, eb):
                    qi, bi, k0, KW, t = u
                    q0, QW, nqb, oslot = qinfo[qi]

                    def mav(e):
                        for j in range(nqb):
                            qw = min(128, QW - j * 128)
                            bko, off, first = oslot[t * nqb + j]
                            ins = e.matmul(bko.ap[:qw, off:off + 129], eb.ap[:KW, j * 128:j * 128 + qw], V1[:KW, bi, 0:129],
                                           start=(bi == 0 and first), stop=(bi == nb - 1), skip_group_check=True)
                        return ins
                    P.op("pe", mav, reads=[eb.dep, vd], writes=odeps)
                    if bi == nb - 1 and t == 1:
                        epilogue(qi)

                def epilogue(qi):
                    q0, QW, nqb, oslot = qinfo[qi]
                    ssq = ssq_all[:, epi_n[0] % 8, :]
                    epi_n[0] += 1
                    nacc = 2 * nqb
                    QP = min(128, QW)

                    def cp(bix, bko, na):
                        P.op("dve", lambda e: e.tensor_copy(Oc[:QP, 3 * bix:3 * bix + na, :].rearrange("p a b -> p (a b)"), bko.ap[:QP, 0:129 * na]),
                             reads=[bko.dep], writes=[ocd])
                    for bix, bko in enumerate(obanks):
                        na = min(3, nacc - 3 * bix)
                        if na > 0:
                            cp(bix, bko, na)
                    oos = []

                    def blk1(j):
                        qw = min(128, QW - j * 128)
                        i1, i2 = j, nqb + j
                        tt = t1.next()
                        oo = o2.next()
                        P.op("dve", lambda e: e.reciprocal(rec[:qw, i1, 0:1], Oc[:qw, i1, 128:129]), reads=[ocd], writes=[ocd])
                        P.op("dve", lambda e: e.reciprocal(rec[:qw, i1, 1:2], Oc[:qw, i2, 128:129]), reads=[ocd], writes=[ocd])
                        P.op("dve", lambda e: e.tensor_tensor(rec[:qw, i1, 1:2], rec[:qw, i1, 1:2], neglam[:qw, :], ALU.mult), reads=[ocd, cdep], writes=[ocd])
                        P.op("dve", lambda e: e.tensor_scalar(tt.ap[:qw, :], Oc[:qw, i1, 0:128], rec[:qw, i1, 0:1], None, ALU.mult), reads=[ocd], writes=[tt.dep])
                        P.op("dve", lambda e: e.scalar_tensor_tensor(oo.ap[:qw, :], Oc[:qw, i2, 0:128], rec[:qw, i1, 1:2], tt.ap[:qw, :], ALU.mult, ALU.add),
                             reads=[ocd, tt.dep], writes=[oo.dep])
                        P.op("dve", lambda e: e.tensor_tensor(junk[:qw, :], oo.ap[:qw, :], oo.ap[:qw, :], ALU.mult), reads=[oo.dep], writes=[ocd])
                        P.op("dve", lambda e: e.reduce_sum(ssq[:qw, j:j + 1], junk[:qw, :], axis=mybir.AxisListType.X), reads=[ocd], writes=[ssqd])
                        oos.append(oo)
                    for j in range(nqb):
                        blk1(j)

                    def stage2():
                        P.op("act", lambda e: e.activation(ssq[:QP, 0:nqb], ssq[:QP, 0:nqb], AF.Ln, bias=epsb[:QP, :], scale=1.0 / 128.0), reads=[ssqd, cdep], writes=[ssqd])
                        P.op("act", lambda e: e.activation(ssq[:QP, 0:nqb], ssq[:QP, 0:nqb], AF.Exp, scale=-0.5), reads=[ssqd], writes=[ssqd])
                    abs_ = []

                    def stage3a():
                        def one(j):
                            qw = min(128, QW - j * 128)
                            ab = attb.next()
                            oo = oos[j]
                            P.op("dve", lambda e: e.scalar_tensor_tensor(ab.ap[:qw, :], oo.ap[:qw, :], ssq[:qw, j:j + 1], subw[:qw, :], ALU.mult, ALU.mult),
                                 reads=[ssqd, oo.dep, cdep], writes=[ab.dep])
                            abs_.append(ab)
                        for j in range(nqb):
                            one(j)

                    def stage3b():
                        def mtr(e):
                            for j in range(nqb):
                                qw = min(128, QW - j * 128)
                                ins = e.transpose(tp_bf[:, j * 128:j * 128 + qw], abs_[j].ap[:qw, :], ident[:qw, :qw])
                            return ins
                        P.op("pe", mtr, reads=[a_.dep for a_ in abs_] + [cdep], writes=[tbank.dep])

                    def stage4():
                        P.op("dve", lambda e: e.tensor_copy(MIX[:, h, q0:q0 + QW], tp_bf[:, 0:QW]), reads=[tbank.dep], writes=[mix_dep[h]])
                    defer(4 * EPI_DELAY, stage2)
                    defer(8 * EPI_DELAY, stage3a)
                    defer(12 * EPI_DELAY, stage3b)
                    defer(16 * EPI_DELAY, stage4)

                ebs = {}
                nxt = 0
                for i in range(len(units)):
                    while nxt < min(i + LA + 1, len(units)):
                        ebs[nxt] = front(units[nxt])
                        nxt += 1
                    back(units[i], ebs.pop(i))
                    gstate["gi"] += 1
                    run_pending()

            wb = load_w(wring, "watt", 0)
            for h in range(8):
                nwb = load_w(wring, "watt", h + 1) if h + 1 < 8 else None
                wdb = wdt_ring.next()
                zflat = zscr[kind][h]
                wd_src = bass.AP(tensor=zflat.tensor, offset=zflat.offset + 128, ap=[[ZW - 1, 128], [1, 1152]])
                P.dma("sp", lambda e, wdb=wdb, wd_src=wd_src: e.dma_start(out=wdb.ap, in_=wd_src), wdb.dsem,
                      reads=[zdep[kind][h]], writes=[wdb.dep])
                for (t0, T) in qtiles:
                    bk = bank(spool)

                    def mq(e, bk=bk, wb=wb, t0=t0, T=T):
                        for ch in range(8):
                            ins = e.matmul(bk.ap[:, :T], wb.ap[:, ch, 0:128], U[:, ch, t0:t0 + T], start=(ch == 0), stop=(ch == 7))
                        return ins
                    P.op("pe", mq, reads=[wb.dep] + U_all, writes=[bk.dep])
                    P.op("act", lambda e, bk=bk, t0=t0, T=T: e.copy(qz[0:64, 0, t0:t0 + T], bk.ap[0:64, :T]), reads=[bk.dep], writes=[qd])
                    P.op("act", lambda e, bk=bk, t0=t0, T=T: e.copy(qz[64:128, 1, t0:t0 + T], bk.ap[64:128, :T]), reads=[bk.dep], writes=[qd])
                for (t0, T) in ktiles:
                    bk = bank(spool)

                    def mk(e, bk=bk, wb=wb, t0=t0, T=T):
                        for ch in range(8):
                            ins = e.matmul(bk.ap[:, :T], wb.ap[:, ch, 128:256], U[:, ch, t0:t0 + T], start=(ch == 0), stop=(ch == 7))
                        return ins
                    P.op("pe", mk, reads=[wb.dep] + U_all, writes=[bk.dep])
                    P.op("dve", lambda e, bk=bk, t0=t0, T=T: e.tensor_copy(kT[:, t0:t0 + T], bk.ap[:, :T]), reads=[bk.dep], writes=[kd])
                for bi, (b0, KW) in enumerate(blks):
                    bk = bank(spool)

                    def mv(e, bk=bk, wb=wb, b0=b0, KW=KW):
                        for ch in range(8):
                            ins = e.matmul(bk.ap[:KW, 0:128], U[:, ch, b0:b0 + KW], wb.ap[:, ch, 256:384], start=(ch == 0), stop=(ch == 7))
                        return ins
                    P.op("pe", mv, reads=[wb.dep] + U_all, writes=[bk.dep])
                    eng = "act" if bi % 2 == 0 else "dve"
                    if eng == "act":
                        P.op("act", lambda e, bk=bk, bi=bi, KW=KW: e.copy(V1[:KW, bi, 0:128], bk.ap[:KW, 0:128]), reads=[bk.dep], writes=[vd])
                    else:
                        P.op("dve", lambda e, bk=bk, bi=bi, KW=KW: e.tensor_copy(V1[:KW, bi, 0:128], bk.ap[:KW, 0:128]), reads=[bk.dep], writes=[vd])
                attend_head(h, wdb)
                if nwb is not None:
                    wb = nwb
            run_pending(force=True)

        def ssm_phase(kind, L, U, U_all, MIX, mix_dep):
            wk = Bump(arena, [(R0, 8 * LO * 2), (R3, R3SZ)])
            blks = tiles_of(L, 128)
            nb = len(blks)
            nbo = len(tiles_of(LO, 128))
            vk = vec[kind]
            dtt = wk.alloc([nb, 32], F32)
            aa = wk.alloc([nb, 32], F32)
            ncum = wk.alloc([nb, 32], F32)
            wgt = wk.alloc([nb, 32], F32)
            etot = wk.alloc([nb, 32], F32)
            LOB = min(L, (LO + 127) // 128 * 128)
            BT = wk.alloc([LOB], BF16)
            CT = wk.alloc([LOB], BF16)
            Btok = wk.alloc([nb, 128], BF16)
            xT = wk.alloc([L], BF16)
            xtok = wk.alloc([nb, 128], BF16)
            zs = wk.alloc([LO], BF16)
            Hbin = wk.alloc([nbo, 2, 64], BF16)
            Hst = wk.alloc([2, 2, 64], F32)
            wdtb = Buf(wk.alloc([8, 32], BF16), P.dep(), P.dsem("wdt"))
            wzring = Ring([Buf(wk.alloc([8, 256], BF16), P.dep(), P.dsem("wzx%d" % i)) for i in range(2)])
            stg = Ring([Buf(wk.alloc([520], BF16), P.dep()) for _ in range(2)])
            dgr = Ring([Buf(wk.alloc([7, 128], BF16), P.dep()) for _ in range(2)])
            abr = Ring([Buf(wk.alloc([4, 128], F32), P.dep()) for _ in range(2)])
            dmr = Ring([Buf(wk.alloc([4, 128], F32), P.dep()) for _ in range(2)])
            ebr = Ring([Buf(wk.alloc([4, 128], F32), P.dep()) for _ in range(2)])
            wtr = Ring([Buf(wk.alloc([4, 128], BF16), P.dep()) for _ in range(3)])
            cer = Ring([Buf(wk.alloc([4, 128], BF16), P.dep()) for _ in range(3)])
            bwr = Ring([Buf(wk.alloc([2, 64], BF16), P.dep()) for _ in range(3)])
            gmr = Ring([Buf(wk.alloc([2, 128], BF16), P.dep()) for _ in range(2)])
            yvr = Ring([Buf(wk.alloc([128], F32), P.dep()) for _ in range(2)])
            hbr = Ring([Buf(wk.alloc([2, 64], BF16), P.dep()) for _ in range(3)])
            dtd, bcd, xd, zd, hd_, hbd = P.dep(), P.dep(), P.dep(), P.dep(), P.dep(), P.dep()
            btd = P.dep()
            pl = [0, 1, 2]
            cring = Ring([banks[0], banks[1]])
            gring = Ring([banks[2]])
            yring = Ring([banks[4], banks[5]])
            string = Ring([banks[6], banks[7]])
            tpring = Ring([banks[3], banks[4]])
            wname = "wdt" + kind

            wbcb = load_w(wzring, "wbc", 0)
            P.dma("sp", lambda e: e.dma_start(out=wdtb.ap, in_=wbf[wname][0].rearrange("p (a b) -> p a b", a=8)), wdtb.dsem,
                  reads=[wdep[wname][0]], writes=[wdtb.dep])

            for g0 in range(0, nb, 16):
                bk = bank(pl)
                gn = min(16, nb - g0)

                def mdt(e, bk=bk, g0=g0, gn=gn):
                    for i in range(gn):
                        b0, KW = blks[g0 + i]
                        for ch in range(8):
                            ins = e.matmul(bk.ap[:KW, i * 32:(i + 1) * 32], U[:, ch, b0:b0 + KW], wdtb.ap[:, ch, :], start=(ch == 0), stop=(ch == 7))
                    return ins
                P.op("pe", mdt, reads=[wdtb.dep] + U_all, writes=[bk.dep])
                for i in range(gn):
                    b0, KW = blks[g0 + i]
                    P.op("dve", lambda e, bk=bk, i=i, g0=g0, KW=KW: e.tensor_tensor(dtt[:KW, g0 + i, :], bk.ap[:KW, i * 32:(i + 1) * 32],
                                                                                     row[kind][:KW, 0:32], ALU.add),
                         reads=[bk.dep, cdep], writes=[dtd])
            nfull = nb - 1
            lastKW = blks[-1][1]

            fl = lambda a: a[:, 0:nfull, :]
            ll = lambda a: a[:lastKW, nfull:nb, :]
            for sel in (fl, ll):
                P.op("act", lambda e, sel=sel: e.activation(sel(dtt), sel(dtt), AF.Exp), reads=[dtd], writes=[dtd])
                P.op("act", lambda e, sel=sel: e.activation(sel(dtt), sel(dtt), AF.Ln, bias=1.0), reads=[dtd], writes=[dtd])
            P.op("dve", lambda e: e.tensor_tensor(fl(aa), fl(dtt), negA[kind].unsqueeze(1).to_broadcast([128, nfull, 32]), ALU.mult),
                 reads=[dtd, cdep], writes=[dtd])
            P.op("dve", lambda e: e.tensor_tensor(ll(aa), ll(dtt), negA[kind][:lastKW, :].unsqueeze(1).to_broadcast([lastKW, 1, 32]), ALU.mult),
                 reads=[dtd, cdep], writes=[dtd])
            for g0 in range(0, nb, 8):
                bk = bank(pl)
                bk2 = bank(pl)
                gn = min(8, nb - g0)

                def mcum(e, bk=bk, bk2=bk2, g0=g0, gn=gn):
                    for i in range(gn):
                        b0, KW = blks[g0 + i]
                        e.matmul(bk.ap[:KW, i * 32:i * 32 + 16], trif[:KW, :KW], aa[:KW, g0 + i, 0:16], start=True, stop=True)
                        e.matmul(bk.ap[:KW, i * 32 + 16:i * 32 + 32], trib[:KW, :KW], aa[:KW, g0 + i, 16:32], start=True, stop=True)
                        ins = e.matmul(bk2.ap[:, i * 32:i * 32 + 32], onesf[:KW, :], aa[:KW, g0 + i, :], start=True, stop=True)
                    return ins
                P.op("pe", mcum, reads=[dtd, cdep], writes=[bk.dep, bk2.dep])
                for i in range(gn):
                    b0, KW = blks[g0 + i]
                    P.op("dve", lambda e, bk=bk, bk2=bk2, i=i, g0=g0, KW=KW: e.tensor_copy(etot[:, g0 + i, :], bk2.ap[:, i * 32:i * 32 + 32]),
                         reads=[bk2.dep], writes=[dtd])
                    P.op("dve", lambda e, bk=bk, i=i, g0=g0, KW=KW: e.tensor_scalar(ncum[:KW, g0 + i, :], bk.ap[:KW, i * 32:i * 32 + 32], -1.0, None, ALU.mult),
                         reads=[bk.dep], writes=[dtd])
                    P.op("dve", lambda e, i=i, g0=g0, KW=KW: e.tensor_tensor(wgt[:KW, g0 + i, :], etot[:KW, g0 + i, :], ncum[:KW, g0 + i, :], ALU.add),
                         reads=[dtd], writes=[dtd])
            for sel in (fl, ll):
                P.op("act", lambda e, sel=sel: e.activation(sel(wgt), sel(wgt), AF.Exp), reads=[dtd], writes=[dtd])
                P.op("dve", lambda e, sel=sel: e.tensor_tensor(sel(wgt), sel(wgt), sel(dtt), ALU.mult), reads=[dtd], writes=[dtd])
            for sel in (fl, ll):
                P.op("act", lambda e, sel=sel: e.activation(sel(dtt), sel(dtt), AF.Ln), reads=[dtd], writes=[dtd])
                P.op("dve", lambda e, sel=sel: e.tensor_tensor(sel(ncum), sel(ncum), sel(dtt), ALU.add), reads=[dtd], writes=[dtd])
            P.op("act", lambda e: e.activation(etot, etot, AF.Exp), reads=[dtd], writes=[dtd])

            ctl = tiles_of(L, 506)

            def conv_chunk(wbuf, col0, cc, emit_out):
                dg = dgr.next()

                def mk_dg(k):
                    P.op("dve", lambda e: e.tensor_scalar(dg.ap[:, k, :], tmpf, vk[:, cc * 7 + k:cc * 7 + k + 1], None, ALU.mult), reads=[cdep], writes=[dg.dep])
                for k in range(7):
                    mk_dg(k)

                def tile(t0, T):
                    lo = max(t0 - 3, 0)
                    hi = min(t0 + T + 3, L)
                    n = hi - lo
                    bk = bank(pl)

                    def mp(e):
                        for ch in range(8):
                            ins = e.matmul(bk.ap[:, :n], wbuf.ap[:, ch, col0:col0 + 128], U[:, ch, lo:lo + n], start=(ch == 0), stop=(ch == 7))
                        return ins
                    P.op("pe", mp, reads=[wbuf.dep] + U_all, writes=[bk.dep])
                    sg = stg.next()
                    off = lo - (t0 - 3)
                    if off > 0:
                        P.op("pool", lambda e: e.memset(sg.ap[:, 0:off], 0.0), writes=[sg.dep])
                    if off + n < T + 6:
                        P.op("pool", lambda e: e.memset(sg.ap[:, off + n:T + 6], 0.0), writes=[sg.dep])
                    P.op("act", lambda e: e.copy(sg.ap[:, off:off + n], bk.ap[:, :n]), reads=[bk.dep], writes=[sg.dep])
                    co = bank(pl)

                    def mc(e):
                        for k in range(7):
                            ins = e.matmul(co.ap[:, :T], dg.ap[:, k, :], sg.ap[:, k:k + T], start=(k == 0), stop=(k == 6))
                        return ins
                    P.op("pe", mc, reads=[sg.dep, dg.dep], writes=[co.dep])
                    emit_out(co, t0, T)
                for (t0, T) in ctl:
                    tile(t0, T)

            def transpose_all(src, sdep, dst, ddep):
                def group(bis):
                    tb = tpring.next()
                    tpv = tb.ap[:, 0:256].bitcast(BF16)
                    kw = blks[bis[0]][1]

                    def mt(e):
                        for i, bi in enumerate(bis):
                            b0, KW = blks[bi]
                            ins = e.transpose(tpv[:KW, i * 128:(i + 1) * 128], src[:, b0:b0 + KW], ident)
                        return ins
                    P.op("pe", mt, reads=[sdep, cdep], writes=[tb.dep])
                    ng = len(bis)
                    P.op("dve", lambda e: e.tensor_copy(dst[:kw, bis[0]:bis[0] + ng, :], tpv[:kw, 0:128 * ng].rearrange("p (a b) -> p a b", a=ng)),
                         reads=[tb.dep], writes=[ddep])
                full = [bi for bi in range(nb) if blks[bi][1] == 128]
                for i in range(0, len(full), 4):
                    group(full[i:i + 4])
                for bi in range(nb):
                    if blks[bi][1] != 128:
                        group([bi])

            def out_B(co, t0, T):
                P.op("act", lambda e: e.activation(xT[:, t0:t0 + T], co.ap[:, :T], AF.Silu, bias=vk[:, 78:79]), reads=[co.dep, cdep], writes=[xd])
            conv_chunk(wbcb, 0, 8, out_B)
            transpose_all(xT, xd, Btok, btd)
            P.op("dve", lambda e: e.tensor_copy(BT, xT[:, 0:LOB]), reads=[xd], writes=[bcd])

            def out_C(co, t0, T):
                T2 = min(T, LOB - t0)
                if T2 > 0:
                    P.op("act", lambda e: e.activation(CT[:, t0:t0 + T2], co.ap[:, :T2], AF.Silu, bias=vk[:, 79:80]), reads=[co.dep, cdep], writes=[bcd])
            conv_chunk(wbcb, 128, 9, out_C)

            def do_pair(j, wz):
                g = j // 4
                gs = slice(g * 64, g * 64 + 64)

                def out_x(co, t0, T):
                    P.op("act", lambda e: e.activation(xT[:, t0:t0 + T], co.ap[:, :T], AF.Silu, bias=vk[:, 70 + j:71 + j]), reads=[co.dep, cdep], writes=[xd])
                conv_chunk(wz, 128, j, out_x)
                transpose_all(xT, xd, xtok, xd)

                def do_z(t0, T):
                    bk = bank(pl)

                    def mz(e):
                        for ch in range(8):
                            ins = e.matmul(bk.ap[:, :T], wz.ap[:, ch, 0:128], U[:, ch, t0:t0 + T], start=(ch == 0), stop=(ch == 7))
                        return ins
                    P.op("pe", mz, reads=[wz.dep] + U_all, writes=[bk.dep])
                    P.op("act", lambda e: e.activation(zs[:, t0:t0 + T], bk.ap[:, :T], AF.Silu), reads=[bk.dep], writes=[zd])
                for (t0, T) in tiles_of(LO):
                    do_z(t0, T)

                P.op("pool", lambda e: e.memset(Hst, 0.0), writes=[hd_])
                hb0 = hbr.next()
                P.op("pool", lambda e: e.memset(hb0.ap, 0.0), writes=[hb0.dep])

                def st_front(bi, dirn):
                    b0, KW = blks[bi]
                    c0 = dirn * 16 + 2 * j
                    bw = bwr.next()
                    P.op("dve", lambda e: e.tensor_tensor(bw.ap[:KW, :, :], Btok[:KW, bi, g * 64:g * 64 + 64].unsqueeze(1).to_broadcast([KW, 2, 64]),
                                                          wgt[:KW, bi, c0:c0 + 2].unsqueeze(2).to_broadcast([KW, 2, 64]), ALU.mult),
                         reads=[btd, dtd], writes=[bw.dep])
                    sb_ = string.next()

                    def mst(e):
                        e.matmul(sb_.ap[gs, 0:64], bw.ap[:KW, 0, :], xtok[:KW, bi, 0:64], start=True, stop=True)
                        return e.matmul(sb_.ap[gs, 64:128], bw.ap[:KW, 1, :], xtok[:KW, bi, 64:128], start=True, stop=True)
                    P.op("pe", mst, reads=[bw.dep, xd], writes=[sb_.dep])
                    return sb_

                def st_update(bi, dirn, sb_):
                    c0 = dirn * 16 + 2 * j

                    def one(e_):
                        P.op("dve", lambda e: e.scalar_tensor_tensor(Hst[gs, dirn, e_, :], Hst[gs, dirn, e_, :], etot[gs, bi, c0 + e_:c0 + e_ + 1],
                                                                     sb_.ap[gs, e_ * 64:e_ * 64 + 64], ALU.mult, ALU.add),
                             reads=[sb_.dep, hd_, dtd], writes=[hd_])
                    one(0)
                    one(1)

                def save_hbin(bi):
                    P.op("dve", lambda e: e.tensor_copy(Hbin[gs, bi, :, :], Hst[gs, 1, :, :]), reads=[hd_], writes=[hbd])

                border = list(range(nb - 1, -1, -1))
                bsb = {}
                bbw = {}

                def b1(bi):
                    b0, KW = blks[bi]
                    c0 = 16 + 2 * j
                    bw = bwr.next()
                    P.op("dve", lambda e: e.tensor_tensor(bw.ap[:KW, :, :], Btok[:KW, bi, g * 64:g * 64 + 64].unsqueeze(1).to_broadcast([KW, 2, 64]),
                                                          wgt[:KW, bi, c0:c0 + 2].unsqueeze(2).to_broadcast([KW, 2, 64]), ALU.mult),
                         reads=[btd, dtd], writes=[bw.dep])
                    bbw[bi] = bw

                def st_mm(bi, bw):
                    b0, KW = blks[bi]
                    sb_ = string.next()

                    def mst(e):
                        e.matmul(sb_.ap[gs, 0:64], bw.ap[:KW, 0, :], xtok[:KW, bi, 0:64], start=True, stop=True)
                        return e.matmul(sb_.ap[gs, 64:128], bw.ap[:KW, 1, :], xtok[:KW, bi, 64:128], start=True, stop=True)
                    P.op("pe", mst, reads=[bw.dep, xd], writes=[sb_.dep])
                    return sb_

                def b2(bi):
                    bsb[bi] = st_mm(bi, bbw.pop(bi))

                def b3(bi):
                    if bi < nbo:
                        save_hbin(bi)
                    if bi > 0:
                        st_update(bi, 1, bsb.pop(bi))
                for k in range(-2, len(border)):
                    if 0 <= k < len(border):
                        b3(border[k])
                    if 0 <= k + 1 < len(border) and border[k + 1] > 0:
                        b2(border[k + 1])
                    if 0 <= k + 2 < len(border) and border[k + 2] > 0:
                        b1(border[k + 2])

                cx = {}

                def s1(c):
                    b0, KW = blks[c]
                    d = cx[c] = {}
                    abc = d["abc"] = abr.next()

                    def mk_abc(dirn):
                        c0 = dirn * 16 + 2 * j
                        P.op("dve", lambda e: e.tensor_copy(abc.ap[:KW, 2 * dirn:2 * dirn + 2, :], aa[:KW, c, c0:c0 + 2].unsqueeze(2).to_broadcast([KW, 2, 128])),
                             reads=[dtd], writes=[abc.dep])
                    mk_abc(0)
                    mk_abc(1)
                    if c + 1 < nbo:
                        bw = d["bw"] = bwr.next()
                        c0 = 2 * j
                        P.op("dve", lambda e: e.tensor_tensor(bw.ap[:KW, :, :], Btok[:KW, c, g * 64:g * 64 + 64].unsqueeze(1).to_broadcast([KW, 2, 64]),
                                                              wgt[:KW, c, c0:c0 + 2].unsqueeze(2).to_broadcast([KW, 2, 64]), ALU.mult),
                             reads=[btd, dtd], writes=[bw.dep])

                def s2(c):
                    b0, KW = blks[c]
                    d = cx[c]
                    gbk = d["gb"] = gring.next()
                    P.op("pe", lambda e: e.matmul(gbk.ap[:KW, :KW], BT[gs, b0:b0 + KW], CT[gs, b0:b0 + KW], start=True, stop=True),
                         reads=[bcd], writes=[gbk.dep])
                    cb_ = d["cb"] = cring.next()
                    abc = d["abc"]

                    def mcb(e):
                        for u in range(4):
                            tri = trif if u < 2 else trib
                            ins = e.matmul(cb_.ap[:, u * 128:u * 128 + KW], abc.ap[:KW, u, :], tri[:KW, :KW], start=True, stop=True)
                        return ins
                    P.op("pe", mcb, reads=[abc.dep, cdep], writes=[cb_.dep])
                    if "bw" in d:
                        d["sb"] = st_mm(c, d["bw"])

                def s3(c):
                    b0, KW = blks[c]
                    d = cx[c]
                    gm = d["gm"] = gmr.next()
                    gbk = d["gb"]
                    P.op("dve", lambda e: e.tensor_tensor(gm.ap[:KW, :, :KW], gbk.ap[:KW, :KW].unsqueeze(1).to_broadcast([KW, 2, KW]), tri2[:KW, :, :KW], ALU.mult),
                         reads=[gbk.dep, cdep], writes=[gm.dep])
                    dm = d["dm"] = dmr.next()
                    cb_ = d["cb"]

                    def mk_dm(u):
                        hcol = (u // 2) * 16 + 2 * j + (u % 2)
                        P.op("act", lambda e: e.activation(dm.ap[:KW, u, :KW], cb_.ap[:KW, u * 128:u * 128 + KW], AF.Exp, bias=ncum[:KW, c, hcol:hcol + 1]),
                             reads=[cb_.dep, dtd], writes=[dm.dep])
                    for u in range(4):
                        mk_dm(u)
                    eb = d["eb"] = ebr.next()
                    P.op("act", lambda e: e.activation(eb.ap[:, :, :KW], cb_.ap.rearrange("p (u k) -> p u k", u=4)[:, :, :KW], AF.Exp), reads=[cb_.dep], writes=[eb.dep])

                def s4(c):
                    b0, KW = blks[c]
                    d = cx[c]
                    wt = d["wt"] = wtr.next()
                    dm, gm, eb = d["dm"], d["gm"], d["eb"]

                    def mk_wt(dirn):
                        P.op("dve", lambda e: e.scalar_tensor_tensor(wt.ap[:KW, 2 * dirn:2 * dirn + 2, :KW], dm.ap[:KW, 2 * dirn:2 * dirn + 2, :KW], 1e30,
                                                                     gm.ap[:KW, dirn, :KW].unsqueeze(1).to_broadcast([KW, 2, KW]), ALU.min, ALU.mult),
                             reads=[dm.dep, gm.dep], writes=[wt.dep])
                    mk_wt(0)
                    mk_wt(1)
                    ce = d["ce"] = cer.next()
                    P.op("dve", lambda e: e.tensor_tensor(ce.ap[gs, :, :KW], eb.ap[gs, :, :KW], CT[gs, b0:b0 + KW].unsqueeze(1).to_broadcast([64, 4, KW]), ALU.mult),
                         reads=[eb.dep, bcd], writes=[ce.dep])

                def s5(c):
                    d = cx[c]
                    if c + 1 < nbo:
                        st_update(c, 0, d["sb"])
                        hb2 = hbr.next()
                        P.op("dve", lambda e: e.tensor_copy(hb2.ap[gs, :, :], Hst[gs, 0, :, :]), reads=[hd_], writes=[hb2.dep])
                        cx.setdefault(c + 1, {})
                        hbs[c + 1] = hb2

                def s6(c):
                    b0, KW = blks[c]
                    d = cx[c]
                    wt, ce, hb = d["wt"], d["ce"], hbs[c]
                    yb = d["yb"] = yring.next()

                    def my(e):
                        for e_ in range(2):
                            o = yb.ap[e_ * 64:e_ * 64 + 64, :KW]
                            xs_ = xtok[:KW, c, e_ * 64:e_ * 64 + 64]
                            e.matmul(o, xs_, wt.ap[:KW, e_, :KW], start=True, stop=False)
                            e.matmul(o, xs_, wt.ap[:KW, 2 + e_, :KW], start=False, stop=False)
                            e.matmul(o, hb.ap[gs, e_, :], ce.ap[gs, e_, :KW], start=False, stop=False)
                            ins = e.matmul(o, Hbin[gs, c, e_, :], ce.ap[gs, 2 + e_, :KW], start=False, stop=True)
                        return ins
                    P.op("pe", my, reads=[xd, hb.dep, hbd, wt.dep, ce.dep], writes=[yb.dep])

                def s7(c):
                    b0, KW = blks[c]
                    yb = cx[c]["yb"]
                    yv = yvr.next()
                    P.op("dve", lambda e: e.scalar_tensor_tensor(yv.ap[:, :KW], xT[:, b0:b0 + KW], vk[:, 80 + j:81 + j], yb.ap[:, :KW], ALU.mult, ALU.add),
                         reads=[yb.dep, xd, cdep], writes=[yv.dep])
                    KO = min(KW, LO - b0)
                    P.op("dve", lambda e: e.tensor_tensor(MIX[:, 8 + j, b0:b0 + KO], yv.ap[:, :KO], zs[:, b0:b0 + KO], ALU.mult),
                         reads=[yv.dep, zd], writes=[mix_dep[8 + j]])
                    del cx[c]

                hbs = {0: hb0}
                stages = [(s7, -2), (s6, -1), (s5, 0), (s4, 0), (s3, 1), (s2, 2), (s1, 3)]
                for i in range(-3, nbo + 2):
                    for fn, off in stages:
                        c = i + off
                        if 0 <= c < nbo:
                            fn(c)

            wz = load_w(wzring, "wzx", 0)
            for j in range(8):
                nwz = load_w(wzring, "wzx", j + 1) if j + 1 < 8 else None
                do_pair(j, wz)
                if nwz is not None:
                    wz = nwz

        out_dep = P.dep()
        if do_S:
            run_seq("S", xs, LS, ys)
        if do_P:
            run_seq("P", xp, LP, yp)
        P.op("sp", None, reads=[out_dep])
        P.op("pool", None, reads=[out_dep])
        P.emit(nc, st)
    return nc


def _oh_const():
    m = np.arange(ZW)
    b = t5_bucket_np(640 - m)
    oh = np.zeros((32, ZW), np.float32)
    oh[b, m] = 1.0
    return oh


def _pc(v):
    v = np.asarray(v, np.float32).reshape(-1, 128)
    return np.ascontiguousarray(v.T)


def prep_inputs(inp):
    f = lambda a: np.ascontiguousarray(np.asarray(a, np.float32))
    w_in = f(inp["w_in"])[0]
    common = {}

    def gu(w):
        return np.ascontiguousarray(w.reshape(8, 128, 11, 256).transpose(2, 1, 0, 3).reshape(11, 128, 2048))

    def dn(w):
        return np.ascontiguousarray(w.reshape(22, 128, 8, 128).transpose(2, 1, 0, 3).reshape(8, 128, 2816))

    def cols(w, c0, n):
        return np.ascontiguousarray(w[:, c0:c0 + n].reshape(8, 128, n).transpose(1, 0, 2).reshape(128, 8 * n))
    common["wg1"] = gu(f(inp["ffn1_w_gate"])[0])
    common["wu1"] = gu(f(inp["ffn1_w_up"])[0])
    common["wd1"] = dn(f(inp["ffn1_w_down"])[0])
    common["wg2"] = gu(f(inp["ffn2_w_gate"])[0])
    common["wu2"] = gu(f(inp["ffn2_w_up"])[0])
    common["wd2"] = dn(f(inp["ffn2_w_down"])[0])
    watt = np.zeros((8, 128, 8, 384), np.float32)
    for h in range(8):
        for s, base in enumerate((0, 1024, 2048)):
            watt[h, :, :, s * 128:(s + 1) * 128] = w_in[:, base + h * 128:base + (h + 1) * 128].reshape(8, 128, 128).transpose(1, 0, 2)
    common["watt"] = watt.reshape(8, 128, 3072)
    wzx = np.zeros((8, 128, 8, 256), np.float32)
    for j in range(8):
        wzx[j, :, :, 0:128] = w_in[:, 3072 + j * 128:3072 + (j + 1) * 128].reshape(8, 128, 128).transpose(1, 0, 2)
        wzx[j, :, :, 128:256] = w_in[:, 4096 + j * 128:4096 + (j + 1) * 128].reshape(8, 128, 128).transpose(1, 0, 2)
    common["wzx"] = wzx.reshape(8, 128, 2048)
    common["wbc"] = cols(w_in, 5120, 256)[None]
    wdt_nat = cols(w_in, 5376, 32)[None]
    w_dt_sw = np.concatenate([w_in[:, 5392:5408], w_in[:, 5376:5392]], axis=1)
    wdt_swp = cols(w_dt_sw, 0, 32)[None]
    wo = f(inp["w_out"])[0]
    common["wo"] = np.ascontiguousarray(wo.reshape(16, 128, 8, 128).transpose(2, 1, 0, 3).reshape(8, 128, 2048))
    common["nrm"] = np.concatenate([_pc(inp["ffn1_norm_w"][0]), _pc(inp["mix_norm_w"][0]), _pc(inp["ffn2_norm_w"][0]),
                                    _pc(inp["final_norm_w"]), _pc(inp["ssm_norm_w"][0])], axis=1)
    common["lam"] = np.concatenate([f(inp["lambda_q1"])[0], f(inp["lambda_k1"])[0], f(inp["lambda_q2"])[0], f(inp["lambda_k2"])[0]])[None]
    common["oh"] = _oh_const()
    conv_w = f(inp["conv_w"])[0]
    conv_b = f(inp["conv_b"])[0]
    dsk = np.repeat(f(inp["ssm_d"])[0], 64)

    def vecs(cw):
        v = np.zeros((128, 88), np.float32)
        v[:, 0:70] = cw.T.reshape(10, 128, 7).transpose(1, 0, 2).reshape(128, 70)
        v[:, 70:80] = _pc(conv_b)
        v[:, 80:88] = _pc(dsk)
        return v

    def rows(dbf, dbb, alf, alb):
        r = np.zeros((1, 192), np.float32)
        r[0, 0:16] = dbf
        r[0, 16:32] = dbb
        r[0, 32:48] = alf
        r[0, 48:64] = alb
        r[0, 64:192] = f(inp["attn_subln_w"])[0]
        return r
    dbf, dbb = f(inp["dt_bias_fwd"])[0], f(inp["dt_bias_bwd"])[0]
    alf, alb = f(inp["a_log_fwd"])[0], f(inp["a_log_bwd"])[0]
    nat = {"vec": vecs(conv_w), "row": rows(dbf, dbb, alf, alb), "relb": f(inp["rel_bias"]), "wdt": wdt_nat}
    T = f(inp["rel_bias"])
    Tp = np.concatenate([T[0:1], T[17:32], T[16:17], T[1:16]], axis=0)
    rev = {"vec": vecs(conv_w[::-1]), "row": rows(dbb, dbf, alb, alf), "relb": np.ascontiguousarray(Tp), "wdt": wdt_swp}
    meta = f(inp["meta_tokens"])
    xsamp = f(inp["x_sample"])
    xprom = f(inp["x_prompt"])
    in_maps = []
    for c in range(NCORES):
        m = dict(common)
        hs = np.concatenate([meta, xsamp[c]], axis=0)
        m["xs"] = np.ascontiguousarray(hs.T)
        hp = np.concatenate([meta, xprom[c // 2]], axis=0)
        pk = nat
        if c % 2 == 1:
            hp = hp[::-1]
            pk = rev
        m["xp"] = np.ascontiguousarray(hp.T)
        m["vecS"], m["rowS"], m["relbS"], m["wdtS"] = nat["vec"], nat["row"], nat["relb"], nat["wdt"]
        m["vecP"], m["rowP"], m["relbP"], m["wdtP"] = pk["vec"], pk["row"], pk["relb"], pk["wdt"]
        in_maps.append(m)
    return in_maps


def assemble(results):
    nprom, nsamp = LP - NMETA, LS - NMETA
    half = nprom // 2
    y_prompt = np.zeros((4, nprom, 1024), np.float32)
    y_sample = np.zeros((8, nsamp, 1024), np.float32)
    for c in range(len(results)):
        r = results[c]
        y_sample[c] = r["ys"].T[NMETA:]
        ypT = r["yp"].T
        if c % 2 == 0:
            y_prompt[c // 2, 0:half] = ypT[NMETA:NMETA + half]
        else:
            y_prompt[c // 2, half:nprom] = ypT[0:half][::-1]
    return y_prompt, y_sample


_NC_CACHE = {}


def kernel(**inputs):
    in_maps = prep_inputs(inputs)
    if "nc" not in _NC_CACHE:
        _NC_CACHE["nc"] = build_program()
    nc = _NC_CACHE["nc"]
    res = run_bass_kernel_spmd(nc, in_maps, core_ids=list(range(8)))
    return assemble(res.results)
```

```python
import math
from contextlib import ExitStack
import numpy as np
import concourse.bass as bass
import concourse.mybir as mybir
from concourse.bass_utils import run_bass_kernel_spmd

F32 = mybir.dt.float32
BF16 = mybir.dt.bfloat16
AF = mybir.ActivationFunctionType
ALU = mybir.AluOpType

ENGS = ("pe", "act", "dve", "pool", "sp")
NMETA = 16
LS = 2064
NCORES = 8
import os
DEBUG = os.environ.get('KDEBUG', '')
NOBIAS = bool(int(os.environ.get('NOBIAS', '0')))
EPI_DELAY = int(os.environ.get('EPI_DELAY', '1'))
START_ALL = bool(int(os.environ.get('START_ALL', '0')))
LP = 4112
LO = 2064
EPS = 1e-6
ZW = 1536


class Dep:
    __slots__ = ("w", "r")

    def __init__(self):
        self.w = None
        self.r = []


class DSem:
    __slots__ = ("h", "count", "name")

    def __init__(self, name):
        self.h = None
        self.count = 0
        self.name = name


class Op:
    __slots__ = ("eng", "fn", "waits", "need_inc", "dsem", "dcount", "idx", "cnt")

    def __init__(self, eng, fn):
        self.eng = eng
        self.fn = fn
        self.waits = []
        self.need_inc = False
        self.dsem = None
        self.dcount = 0
        self.cnt = 0


class Prog:
    def __init__(self):
        self.ops = {e: [] for e in ENGS}
        self.dsems = []
        self.alldeps = []

    def dep(self):
        d = Dep()
        self.alldeps.append(d)
        return d

    def deps(self, n):
        return [self.dep() for _ in range(n)]

    def dsem(self, name):
        d = DSem(name)
        self.dsems.append(d)
        return d

    def _add(self, op, reads, writes):
        ws = []
        for d in reads:
            if d.w is not None:
                ws.append(d.w)
        for d in writes:
            if d.w is not None:
                ws.append(d.w)
            ws.extend(d.r)
        seen = set()
        for w in ws:
            if id(w) in seen or w is op:
                continue
            seen.add(id(w))
            op.waits.append(w)
            if w.dsem is None:
                w.need_inc = True
        for d in reads:
            d.r.append(op)
        for d in writes:
            d.w = op
            d.r = []
        op.idx = len(self.ops[op.eng])
        self.ops[op.eng].append(op)
        return op

    def op(self, eng, fn, reads=(), writes=()):
        return self._add(Op(eng, fn), reads, writes)

    def dma(self, queue, fn, dsem, reads=(), writes=()):
        op = Op(queue, fn)
        op.dsem = dsem
        dsem.count += 16
        op.dcount = dsem.count
        return self._add(op, reads, writes)

    def barrier(self):
        deps = list(self.alldeps)
        for e in ENGS:
            self.op(e, None, writes=deps)

    def emit(self, nc, stack):
        sems = {e: stack.enter_context(nc.semaphore("s_" + e)) for e in ENGS}
        for i, d in enumerate(self.dsems):
            d.h = stack.enter_context(nc.semaphore("d%d_%s" % (i, d.name)))
        for e in ENGS:
            c = 0
            for op in self.ops[e]:
                if op.dsem is None and op.need_inc:
                    c += 1
                op.cnt = c
        block = stack.enter_context(nc.Block())
        prog = self

        def run(e, eng):
            waited = {}
            for op in prog.ops[e]:
                for w in op.waits:
                    if w.dsem is not None:
                        key, val, h = ("d", id(w.dsem)), w.dcount, w.dsem.h
                    else:
                        key, val, h = ("e", w.eng), w.cnt, sems[w.eng]
                    if waited.get(key, 0) >= val:
                        continue
                    waited[key] = val
                    eng.wait_ge(h, val)
                if op.fn is None:
                    if op.need_inc:
                        eng.nop().then_inc(sems[e], 1)
                    continue
                ins = op.fn(eng)
                if op.dsem is not None:
                    ins.then_inc(op.dsem.h, 16)
                elif op.need_inc:
                    ins.then_inc(sems[e], 1)

        @block.tensor
        def _(eng):
            run("pe", eng)

        @block.scalar
        def _(eng):
            run("act", eng)

        @block.vector
        def _(eng):
            run("dve", eng)

        @block.gpsimd
        def _(eng):
            run("pool", eng)

        @block.sync
        def _(eng):
            run("sp", eng)


class Buf:
    __slots__ = ("ap", "dep", "dsem")

    def __init__(self, ap, dep, dsem=None):
        self.ap = ap
        self.dep = dep
        self.dsem = dsem


class Ring:
    def __init__(self, bufs):
        self.bufs = bufs
        self.i = 0

    def next(self):
        b = self.bufs[self.i % len(self.bufs)]
        self.i += 1
        return b


class Bump:
    def __init__(self, arena, regions):
        self.arena = arena
        self.regions = [[s, s + n] for s, n in regions]

    def alloc(self, free_shape, dt):
        esz = 2 if dt == BF16 else 4
        n = 1
        for s in free_shape:
            n *= s
        nbytes = (n * esz + 31) // 32 * 32
        for r in self.regions:
            if r[1] - r[0] >= nbytes:
                off = r[0]
                r[0] += nbytes
                ap = self.arena[:, off // 4:(off + nbytes) // 4]
                if dt == BF16:
                    ap = ap.bitcast(BF16)
                ap = ap[:, 0:n]
                if len(free_shape) == 2:
                    ap = ap.rearrange("p (a b) -> p a b", a=free_shape[0])
                elif len(free_shape) == 3:
                    ap = ap.rearrange("p (a b c) -> p a b c", a=free_shape[0], b=free_shape[1])
                return ap
        raise RuntimeError("arena region overflow: need %d bytes; regions %s" % (nbytes, self.regions))


def tiles_of(L, w=512):
    out = []
    t = 0
    while t < L:
        out.append((t, min(w, L - t)))
        t += w
    return out


def t5_bucket_np(rel):
    rel = np.asarray(rel, np.int64)
    half, max_exact = 16, 8
    ret = np.where(rel > 0, half, 0)
    n = np.abs(rel)
    nf = np.maximum(n, 1).astype(np.float32)
    large = max_exact + (np.log(nf / np.float32(max_exact)) / np.float32(math.log(128 / max_exact))
                         * np.float32(half - max_exact)).astype(np.int32)
    large = np.minimum(large, half - 1)
    return ret + np.where(n < max_exact, n, large)


W_SPECS = [
    ("wg1", 11, 2048), ("wu1", 11, 2048), ("wd1", 8, 2816),
    ("watt", 8, 3072), ("wzx", 8, 2048), ("wbc", 1, 2048), ("wdtS", 1, 256), ("wdtP", 1, 256),
    ("wo", 8, 2048),
    ("wg2", 11, 2048), ("wu2", 11, 2048), ("wd2", 8, 2816),
]

R0, R1, R2, R3 = 0, 33024, 66048, 131840
R3SZ = 70912
CN0 = R3 + R3SZ
CNSZ = 8192
ARENA_BYTES = CN0 + CNSZ


def build_program(do_S=True, do_P=True, do_attn=True, do_ssm=True):
    nc = bass.Bass("TRN2", target_bir_lowering=False)
    P = Prog()
    din = {}

    def dram_in(name, shape):
        din[name] = nc.dram_tensor(name, list(shape), F32, kind="ExternalInput").ap()
        return din[name]

    xs = dram_in("xs", [1024, LS])
    xp = dram_in("xp", [1024, LP])
    wsrc = {}
    wbf = {}
    wdep = {}
    for name, ns, el in W_SPECS:
        wsrc[name] = dram_in(name, [ns, 128, el])
        wbf[name] = nc.dram_tensor(name + "_bf", [ns, 128, el], BF16).ap()
        wdep[name] = P.deps(ns)
    nrm_d = dram_in("nrm", [128, 40])
    vec_d = {"S": dram_in("vecS", [128, 88]), "P": dram_in("vecP", [128, 88])}
    row_d = {"S": dram_in("rowS", [1, 192]), "P": dram_in("rowP", [1, 192])}
    lam_d = dram_in("lam", [1, 256])
    relb_d = {"S": dram_in("relbS", [32, 8]), "P": dram_in("relbP", [32, 8])}
    oh_d = dram_in("oh", [32, ZW])
    ys = nc.dram_tensor("ys", [1024, LO], F32, kind="ExternalOutput").ap()
    yp = nc.dram_tensor("yp", [1024, LO], F32, kind="ExternalOutput").ap()
    h1s = nc.dram_tensor("h1scr", [1024, LO], F32).ap()
    dbg_d = nc.dram_tensor("dbg", [128, 16 * LO], BF16, kind="ExternalOutput").ap() if DEBUG else None
    dbgu_d = nc.dram_tensor("dbgu", [128, 8 * (LS if DEBUG == 'S' else LP)], BF16, kind="ExternalOutput").ap() if DEBUG else None
    zscr = {"S": nc.dram_tensor("zscrS", [8, 128, ZW], F32).ap(), "P": nc.dram_tensor("zscrP", [8, 128, ZW], F32).ap()}
    zdep = {"S": P.deps(8), "P": P.deps(8)}

    with ExitStack() as st:
        arena_t = st.enter_context(nc.sbuf_tensor("arena", [128, ARENA_BYTES // 4], F32))
        arena = arena_t[:, :]
        psum_t = st.enter_context(nc.psum_tensor("psum", [128, 4096], F32))
        banks = [Buf(psum_t[:, b * 512:(b + 1) * 512], P.dep()) for b in range(8)]

        def emit_casts(names):
            for name, ns, el in W_SPECS:
                if name not in names:
                    continue
                csem_w = P.dsem("c_" + name)
                for s_ in range(ns):
                    P.dma("pool", (lambda e, o=wbf[name][s_], i=wsrc[name][s_]: e.dma_start(out=o, in_=i)), csem_w, writes=[wdep[name][s_]])
                for d in wdep[name]:
                    d.w.dcount = csem_w.count
        early = ("wg1", "wu1", "wd1")
        emit_casts(early)
        late_casts = [lambda: emit_casts([n for n, _, _ in W_SPECS if n not in early])]

        cb = Bump(arena, [(CN0, CNSZ)])
        ident = cb.alloc([128], BF16)
        onesb = cb.alloc([128], BF16)
        onesf = cb.alloc([128], F32)
        tri2 = cb.alloc([2, 128], F32)
        trif = tri2[:, 0, :]
        trib = tri2[:, 1, :]
        nrm = cb.alloc([40], F32)
        epsb = cb.alloc([1], F32)
        neglam = cb.alloc([1], F32)
        lamt = cb.alloc([256], F32)
        lam2 = cb.alloc([8], F32)
        subw = cb.alloc([128], F32)
        vec = {k: cb.alloc([88], F32) for k in "SP"}
        row = {k: cb.alloc([192], F32) for k in "SP"}
        negA = {k: cb.alloc([32], F32) for k in "SP"}
        cdep = P.dep()
        csem = P.dsem("const")
        tmpf = cb.alloc([128], F32)

        P.op("pool", lambda e: e.memset(onesf, 1.0), writes=[cdep])
        P.op("pool", lambda e: e.memset(epsb, EPS), writes=[cdep])
        P.op("pool", lambda e: e.memset(tmpf, 0.0), writes=[cdep])
        P.op("pool", lambda e: e.affine_select(out=tmpf, in_=tmpf, compare_op=ALU.not_equal, fill=1.0, base=0,
                                               pattern=[[-1, 128]], channel_multiplier=1), reads=[cdep], writes=[cdep])
        P.op("dve", lambda e: e.tensor_copy(ident, tmpf), reads=[cdep], writes=[cdep])
        P.op("dve", lambda e: e.tensor_copy(onesb, onesf), reads=[cdep], writes=[cdep])
        P.op("pool", lambda e: e.affine_select(out=trif, in_=onesf, compare_op=ALU.is_ge, fill=0.0, base=0,
                                               pattern=[[1, 128]], channel_multiplier=-1), reads=[cdep], writes=[cdep])
        P.op("pool", lambda e: e.affine_select(out=trib, in_=onesf, compare_op=ALU.is_ge, fill=0.0, base=0,
                                               pattern=[[-1, 128]], channel_multiplier=1), reads=[cdep], writes=[cdep])
        P.dma("sp", lambda e: e.dma_start(out=nrm, in_=nrm_d), csem, writes=[cdep])
        for k in "SP":
            P.dma("sp", lambda e, k=k: e.dma_start(out=vec[k], in_=vec_d[k]), csem, writes=[cdep])
            P.dma("sp", lambda e, k=k: e.dma_start(out=row[k], in_=row_d[k].partition_broadcast(128)), csem, writes=[cdep])
        P.dma("sp", lambda e: e.dma_start(out=lamt, in_=lam_d.partition_broadcast(128)), csem, writes=[cdep])
        P.op("dve", lambda e: e.tensor_tensor(lamt[:, 0:64], lamt[:, 0:64], lamt[:, 64:128], ALU.mult), reads=[cdep], writes=[cdep])
        P.op("dve", lambda e: e.tensor_tensor(lamt[:, 128:192], lamt[:, 128:192], lamt[:, 192:256], ALU.mult), reads=[cdep], writes=[cdep])
        P.op("dve", lambda e: e.reduce_sum(lam2[:, 0:1], lamt[:, 0:64], axis=mybir.AxisListType.X), reads=[cdep], writes=[cdep])
        P.op("dve", lambda e: e.reduce_sum(lam2[:, 1:2], lamt[:, 128:192], axis=mybir.AxisListType.X), reads=[cdep], writes=[cdep])
        P.op("act", lambda e: e.activation(lam2[:, 2:4], lam2[:, 0:2], AF.Exp), reads=[cdep], writes=[cdep])
        P.op("dve", lambda e: e.scalar_tensor_tensor(neglam, lam2[:, 3:4], -0.2, lam2[:, 2:3], ALU.add, ALU.subtract), reads=[cdep], writes=[cdep])
        P.op("dve", lambda e: e.tensor_scalar(subw, row["S"][:, 64:192], 0.8, None, ALU.mult), reads=[cdep], writes=[cdep])
        for k in "SP":
            P.op("act", lambda e, k=k: e.activation(negA[k], row[k][:, 32:64], AF.Exp), reads=[cdep], writes=[cdep])
            P.op("dve", lambda e, k=k: e.tensor_scalar(negA[k], negA[k], -1.0, None, ALU.mult), reads=[cdep], writes=[cdep])

        def build_bias(kind):
            wb_ = Bump(arena, [(R3, R3SZ)])
            oh = wb_.alloc([ZW], F32)
            relb = wb_.alloc([8], F32)
            relbb = wb_.alloc([8, 128], F32)
            zsb = [wb_.alloc([ZW], F32) for _ in range(2)]
            zd = P.deps(2)
            d0 = P.dep()
            s0 = P.dsem("bias" + kind)
            s1 = P.dsem("biasst" + kind)
            P.dma("sp", lambda e: e.dma_start(out=oh[0:32, :], in_=oh_d), s0, writes=[d0])
            P.dma("sp", lambda e: e.dma_start(out=relb[0:32, :], in_=relb_d[kind]), s0, writes=[d0])
            for h in range(8):
                P.op("dve", lambda e, h=h: e.tensor_copy(relbb[0:32, h, :], relb[0:32, h:h + 1].to_broadcast([32, 128])),
                     reads=[d0], writes=[d0])
            for h in range(8):
                for j in range(3):
                    bk = banks[j]
                    P.op("pe", lambda e, h=h, j=j, bk=bk: e.matmul(bk.ap, relbb[0:32, h, :], oh[0:32, j * 512:(j + 1) * 512],
                                                                 start=True, stop=True), reads=[d0], writes=[bk.dep])
                    P.op("act", lambda e, h=h, j=j, bk=bk: e.copy(zsb[h % 2][:, j * 512:(j + 1) * 512], bk.ap),
                         reads=[bk.dep], writes=[zd[h % 2]])
                P.dma("pool", lambda e, h=h: e.dma_start(out=zscr[kind][h], in_=zsb[h % 2]), s1,
                      reads=[zd[h % 2]], writes=[zdep[kind][h]])

        pool_i = [0]

        def bank(pool):
            b = banks[pool[pool_i[0] % len(pool)]]
            pool_i[0] += 1
            return b

        def load_w(ring, name, s):
            b = ring.next()
            n = 1
            for x in b.ap.shape[1:]:
                n *= x
            src = wbf[name][s]
            dst = b.ap
            if len(b.ap.shape) == 3:
                src = src.rearrange("p (a b) -> p a b", a=b.ap.shape[1])
            P.dma("sp", lambda e: e.dma_start(out=dst, in_=src), b.dsem, reads=[wdep[name][s]], writes=[b.dep])
            return b

        def norm_rstd(src, src_deps, T, sq, sq_dep, rstd, rstd_dep, pool):
            P.op("act", lambda e: e.activation(sq[:, :, :T], src[:, :, :T], AF.Square), reads=src_deps, writes=[sq_dep])
            bk = bank(pool)

            def mm(e):
                for c in range(8):
                    ins = e.matmul(bk.ap[:, :T], onesb, sq[:, c, :T], start=(c == 0), stop=(c == 7))
                return ins
            P.op("pe", mm, reads=[sq_dep, cdep], writes=[bk.dep])
            P.op("act", lambda e: e.activation(rstd[:, :T], bk.ap[:, :T], AF.Sqrt, bias=epsb, scale=1.0 / 1024.0),
                 reads=[bk.dep, cdep], writes=[rstd_dep])
            P.op("dve", lambda e: e.reciprocal(rstd[:, :T], rstd[:, :T]), reads=[rstd_dep], writes=[rstd_dep])

        class FfnCtx:
            pass

        def alloc_ffn(bump, with_wo):
            c = FfnCtx()
            c.hT = Ring([Buf(bump.alloc([8, 512], F32), P.deps(8), P.dsem("hT%d" % i)) for i in range(2)])
            c.u = bump.alloc([8, 512], BF16)
            c.u_dep = P.deps(8)
            c.sq = bump.alloc([8, 512], BF16)
            c.sq_dep = P.dep()
            c.act = bump.alloc([22, 512], BF16)
            c.act_dep = P.deps(22)
            c.sg = Ring([Buf(bump.alloc([512], F32), P.dep()) for _ in range(2)])
            c.rstd = bump.alloc([512], F32)
            c.rstd_dep = P.dep()
            c.wg = Ring([Buf(bump.alloc([8, 256], BF16), P.dep(), P.dsem("wg%d" % i)) for i in range(2)])
            c.wu = Ring([Buf(bump.alloc([8, 256], BF16), P.dep(), P.dsem("wu%d" % i)) for i in range(2)])
            c.wd = Ring([Buf(bump.alloc([22, 128], BF16), P.dep(), P.dsem("wd%d" % i)) for i in range(2)])
            if with_wo:
                c.wo = Ring([Buf(bump.alloc([16, 128], BF16), P.dep(), P.dsem("wo%d" % i)) for i in range(2)])
            c.pool = [0, 1, 2, 3, 4, 5, 6, 7]
            return c

        def ffn(c, hb, T, nk, sfx):
            hT = hb.ap
            norm_rstd(hT, hb.dep, T, c.sq, c.sq_dep, c.rstd, c.rstd_dep, c.pool)
            for ch in range(8):
                P.op("dve", lambda e, ch=ch: e.scalar_tensor_tensor(c.u[:, ch, :T], hT[:, ch, :T], nrm[:, nk * 8 + ch:nk * 8 + ch + 1],
                                                                    c.rstd[:, :T], ALU.mult, ALU.mult),
                     reads=[hb.dep[ch], c.rstd_dep, cdep], writes=[c.u_dep[ch]])
            wgb = load_w(c.wg, "wg" + sfx, 0)
            wub = load_w(c.wu, "wu" + sfx, 0)
            for g in range(11):
                nxt = None
                if g + 1 < 11:
                    nxt = (load_w(c.wg, "wg" + sfx, g + 1), load_w(c.wu, "wu" + sfx, g + 1))
                for k in range(2):
                    f = 2 * g + k
                    pg = bank(c.pool)

                    def mmg(e, wgb=wgb, k=k, pg=pg):
                        for ch in range(8):
                            ins = e.matmul(pg.ap[:, :T], wgb.ap[:, ch, k * 128:(k + 1) * 128], c.u[:, ch, :T], start=(ch == 0), stop=(ch == 7))
                        return ins
                    P.op("pe", mmg, reads=[wgb.dep] + c.u_dep, writes=[pg.dep])
                    pu = bank(c.pool)

                    def mmu(e, wub=wub, k=k, pu=pu):
                        for ch in range(8):
                            ins = e.matmul(pu.ap[:, :T], wub.ap[:, ch, k * 128:(k + 1) * 128], c.u[:, ch, :T], start=(ch == 0), stop=(ch == 7))
                        return ins
                    P.op("pe", mmu, reads=[wub.dep] + c.u_dep, writes=[pu.dep])
                    sg = c.sg.next()
                    P.op("act", lambda e, sg=sg, pg=pg: e.activation(sg.ap[:, :T], pg.ap[:, :T], AF.Silu), reads=[pg.dep], writes=[sg.dep])
                    P.op("dve", lambda e, sg=sg, pu=pu, f=f: e.tensor_tensor(c.act[:, f, :T], sg.ap[:, :T], pu.ap[:, :T], ALU.mult),
                         reads=[sg.dep, pu.dep], writes=[c.act_dep[f]])
                if nxt is not None:
                    wgb, wub = nxt
            wdb = load_w(c.wd, "wd" + sfx, 0)
            for ch in range(8):
                nxt = load_w(c.wd, "wd" + sfx, ch + 1) if ch + 1 < 8 else None
                pd = bank(c.pool)

                def mmd(e, wdb=wdb, pd=pd):
                    for f in range(22):
                        ins = e.matmul(pd.ap[:, :T], wdb.ap[:, f, :], c.act[:, f, :T], start=(f == 0), stop=(f == 21))
                    return ins
                P.op("pe", mmd, reads=[wdb.dep] + c.act_dep, writes=[pd.dep])
                P.op("dve", lambda e, ch=ch, pd=pd: e.scalar_tensor_tensor(hT[:, ch, :T], pd.ap[:, :T], 0.5, hT[:, ch, :T], ALU.mult, ALU.add),
                     reads=[pd.dep, hb.dep[ch]], writes=[hb.dep[ch]])
                if nxt is not None:
                    wdb = nxt

        def run_seq(kind, x_d, L, y_d):
            U = Bump(arena, [(R2, 65792)]).alloc([8, L], BF16)
            U_tdeps = {}
            tl = tiles_of(L)
            for (t0, T) in tl:
                U_tdeps[t0] = P.dep()
            MIX = Bump(arena, [(R0, 66048)]).alloc([16, LO], BF16)
            mix_dep = [P.dep() for _ in range(16)]
            x_v = x_d.rearrange("(c p) t -> p c t", p=128)
            h1_v = h1s.rearrange("(c p) t -> p c t", p=128)
            y_v = y_d.rearrange("(c p) t -> p c t", p=128)
            h1_dep = {t0: P.dep() for (t0, T) in tiles_of(LO)}
            st_sem = P.dsem("store" + kind)

            cA = alloc_ffn(Bump(arena, [(R0, 66048), (R3, R3SZ)]), False)

            def load_x(t0, T):
                hb = cA.hT.next()
                P.dma("sp", lambda e: e.dma_start(out=hb.ap[:, :, :T], in_=x_v[:, :, t0:t0 + T]), hb.dsem, writes=hb.dep)
                return hb
            hb = load_x(*tl[0])
            for i, (t0, T) in enumerate(tl):
                nxt = load_x(*tl[i + 1]) if i + 1 < len(tl) else None
                ffn(cA, hb, T, 0, "1")
                if t0 < LO:
                    T2 = min(T, LO - t0)
                    P.dma("pool", lambda e, hb=hb, t0=t0, T2=T2: e.dma_start(out=h1_v[:, :, t0:t0 + T2], in_=hb.ap[:, :, :T2]), st_sem,
                          reads=hb.dep, writes=[h1_dep[t0]])
                if late_casts:
                    late_casts.pop()()
                norm_rstd(hb.ap, hb.dep, T, cA.sq, cA.sq_dep, cA.rstd, cA.rstd_dep, cA.pool)
                for ch in range(8):
                    P.op("dve", lambda e, ch=ch, hb=hb, t0=t0, T=T: e.scalar_tensor_tensor(U[:, ch, t0:t0 + T], hb.ap[:, ch, :T], nrm[:, 8 + ch:9 + ch],
                                                                                             cA.rstd[:, :T], ALU.mult, ALU.mult),
                         reads=[hb.dep[ch], cA.rstd_dep, cdep], writes=[U_tdeps[t0]])
                hb = nxt
            P.barrier()
            U_all = list(U_tdeps.values())

            if do_ssm:
                ssm_phase(kind, L, U, U_all, MIX, mix_dep)
                P.barrier()
            else:
                P.op("pool", lambda e: e.memset(MIX[:, 8:16, :], 0.0), writes=mix_dep[8:16])
            if do_attn:
                build_bias(kind)
                P.barrier()
                attn_phase(kind, L, U, U_all, MIX, mix_dep)
                P.barrier()
            else:
                P.op("pool", lambda e: e.memset(MIX[:, 0:8, :], 0.0), writes=mix_dep[0:8])
                P.barrier()

            if DEBUG and kind == DEBUG:
                P.dma("pool", lambda e: e.dma_start(out=dbgu_d.rearrange("p (a b) -> p a b", a=8), in_=U), st_sem, reads=U_all, writes=[P.dep()])
                P.dma("pool", lambda e: e.dma_start(out=dbg_d.rearrange("p (a b) -> p a b", a=16), in_=MIX), st_sem, reads=mix_dep, writes=[P.dep()])
            if DEBUG:
                P.barrier()
            c = alloc_ffn(Bump(arena, [(R2, 65792 + R3SZ)]), True)
            tlo = tiles_of(LO)

            def load_h(t0, T):
                hb = c.hT.next()
                P.dma("sp", lambda e: e.dma_start(out=hb.ap[:, :, :T], in_=h1_v[:, :, t0:t0 + T]), hb.dsem, reads=[h1_dep[t0]], writes=hb.dep)
                return hb
            hb = load_h(*tlo[0])
            for i, (t0, T) in enumerate(tlo):
                nxt = load_h(*tlo[i + 1]) if i + 1 < len(tlo) else None
                norm_rstd(MIX[:, 8:16, t0:t0 + T], mix_dep[8:16], T, c.sq, c.sq_dep, c.rstd, c.rstd_dep, c.pool)
                for j in range(8):
                    P.op("dve", lambda e, j=j, t0=t0, T=T: e.scalar_tensor_tensor(c.u[:, j, :T], MIX[:, 8 + j, t0:t0 + T], nrm[:, 32 + j:33 + j],
                                                                                    c.rstd[:, :T], ALU.mult, ALU.mult),
                         reads=[mix_dep[8 + j], c.rstd_dep, cdep], writes=[c.u_dep[j]])
                wob = load_w(c.wo, "wo", 0)
                for ch in range(8):
                    nw = load_w(c.wo, "wo", ch + 1) if ch + 1 < 8 else None
                    po = bank(c.pool)

                    def mmo(e, wob=wob, po=po, t0=t0, T=T):
                        for k in range(8):
                            e.matmul(po.ap[:, :T], wob.ap[:, k, :], MIX[:, k, t0:t0 + T], start=(k == 0), stop=False)
                        for j in range(8):
                            ins = e.matmul(po.ap[:, :T], wob.ap[:, 8 + j, :], c.u[:, j, :T], start=False, stop=(j == 7))
                        return ins
                    P.op("pe", mmo, reads=[wob.dep] + mix_dep[0:8] + c.u_dep, writes=[po.dep])
                    P.op("dve", lambda e, ch=ch, po=po, hb=hb, T=T: e.tensor_tensor(hb.ap[:, ch, :T], hb.ap[:, ch, :T], po.ap[:, :T], ALU.add),
                         reads=[po.dep, hb.dep[ch]], writes=[hb.dep[ch]])
                    if nw is not None:
                        wob = nw
                ffn(c, hb, T, 2, "2")
                norm_rstd(hb.ap, hb.dep, T, c.sq, c.sq_dep, c.rstd, c.rstd_dep, c.pool)
                for ch in range(8):
                    P.op("dve", lambda e, ch=ch, hb=hb, T=T: e.scalar_tensor_tensor(hb.ap[:, ch, :T], hb.ap[:, ch, :T], nrm[:, 24 + ch:25 + ch],
                                                                                      c.rstd[:, :T], ALU.mult, ALU.mult),
                         reads=[hb.dep[ch], c.rstd_dep, cdep], writes=[hb.dep[ch]])
                P.dma("pool", lambda e, hb=hb, t0=t0, T=T: e.dma_start(out=y_v[:, :, t0:t0 + T], in_=hb.ap[:, :, :T]), st_sem,
                      reads=hb.dep, writes=[out_dep])
                hb = nxt
            P.barrier()

        def attn_phase(kind, L, U, U_all, MIX, mix_dep):
            wk = Bump(arena, [(R3, R3SZ)])
            qz = wk.alloc([2, LO], BF16)
            kT = wk.alloc([L], BF16)
            blks = tiles_of(L, 128)
            nb = len(blks)
            V1 = wk.alloc([nb, 132], BF16)
            qd, kd, vd = P.dep(), P.dep(), P.dep()
            ering = Ring([Buf(wk.alloc([512], BF16), P.dep()) for _ in range(6)])
            tring = Ring([Buf(wk.alloc([512], F32), P.dep()) for _ in range(2)])
            Oc = wk.alloc([8, 129], F32)
            ocd = P.dep()
            wring = Ring([Buf(wk.alloc([8, 384], BF16), P.dep(), P.dsem("watt%d" % i)) for i in range(2)])
            wdt_ring = Ring([Buf(wk.alloc([1152], F32), P.dep(), P.dsem("wdtile%d" % i)) for i in range(2)])
            rec = wk.alloc([8, 2], F32)
            t1 = Ring([Buf(wk.alloc([128], F32), P.dep()) for _ in range(2)])
            o2 = Ring([Buf(wk.alloc([128], F32), P.dep()) for _ in range(8)])
            junk = wk.alloc([128], F32)
            ssq_all = wk.alloc([8, 4], F32)
            epi_n = [0]
            attb = Ring([Buf(wk.alloc([128], BF16), P.dep()) for _ in range(8)])
            ssqd = P.dep()
            spool = [0, 1, 6, 7]
            tbank = banks[2]
            obanks = [banks[3], banks[4], banks[5]]
            tp_bf = tbank.ap[:, 0:256].bitcast(BF16)
            P.op("dve", lambda e: e.memset(V1[:, :, 128:130], 1.0), writes=[vd])
            P.op("pool", lambda e: e.memset(qz, 0.0), writes=[qd])
            qtiles = tiles_of(LO)
            ktiles = tiles_of(L)
            rel_lo, rel_hi = -(L + 8), L + 8
            btab = t5_bucket_np(np.arange(rel_lo, rel_hi + 1))

            def classify(k0, KW, q0, QW):
                lo = k0 - (q0 + QW - 1)
                hi = (k0 + KW - 1) - q0
                bs = btab[lo - rel_lo:hi - rel_lo + 1]
                if (bs == 31).all():
                    return "pos"
                if (bs == 15).all():
                    return "neg"
                return "near"

            LA = int(os.environ.get("ATT_LA", "3"))
            gstate = {"gi": 0}
            pending = []

            def run_pending(force=False):
                keep = []
                for due, fn in pending:
                    if force or due <= gstate["gi"]:
                        fn()
                    else:
                        keep.append((due, fn))
                pending[:] = keep

            def defer(delay, fn):
                pending.append((gstate["gi"] + delay, fn))

            def attend_head(h, wdb):
                units = []
                qinfo = {}
                for qi, (q0, QW) in enumerate(qtiles):
                    nqb = (QW + 127) // 128
                    oslot = []
                    for t in range(2):
                        for j in range(nqb):
                            idx = t * nqb + j
                            oslot.append((obanks[idx // 3], (idx % 3) * 129, idx % 3 == 0))
                    qinfo[qi] = (q0, QW, nqb, oslot)
                    for bi, (k0, KW) in enumerate(blks):
                        for t in range(2):
                            units.append((qi, bi, k0, KW, t))
                odeps = [b.dep for b in obanks]

                def front(u):
                    qi, bi, k0, KW, t = u
                    q0, QW, nqb, oslot = qinfo[qi]
                    kindb = classify(k0, KW, q0, QW)
                    d = k0 - q0
                    sbk = bank(spool)
                    P.op("pe", lambda e: e.matmul(sbk.ap[:KW, :QW], kT[:, k0:k0 + KW], qz[:, t, q0:q0 + QW],
                                                  start=True, stop=True), reads=[qd, kd], writes=[sbk.dep])
                    eb = ering.next()
                    if kindb == "near":
                        assert 0 <= 512 - d and 512 - d + QW <= 1152
                        tb = tring.next()
                        P.op("dve", lambda e: e.scalar_tensor_tensor(tb.ap[:KW, :QW], sbk.ap[:KW, :QW], 0.125, wdb.ap[:KW, 512 - d:512 - d + QW],
                                                                     ALU.mult, ALU.add), reads=[sbk.dep, wdb.dep], writes=[tb.dep])
                        P.op("act", lambda e: e.activation(eb.ap[:KW, :QW], tb.ap[:KW, :QW], AF.Exp), reads=[tb.dep], writes=[eb.dep])
                    else:
                        col = 0 if kindb == "pos" else 1151
                        P.op("act", lambda e: e.activation(eb.ap[:KW, :QW], sbk.ap[:KW, :QW], AF.Exp, bias=wdb.ap[:KW, col:col + 1], scale=0.125),
                             reads=[sbk.dep, wdb.dep], writes=[eb.dep])
                    return eb

                def back(u, eb):
                    qi, bi, k0, KW, t = u
                    q0, QW, nqb, oslot = qinfo[qi]

                    def mav(e):
                        for j in range(nqb):
                            qw = min(128, QW - j * 128)
                            bko, off, first = oslot[t * nqb + j]
                            ins = e.matmul(bko.ap[:qw, off:off + 129], eb.ap[:KW, j * 128:j * 128 + qw], V1[:KW, bi, 0:129],
                                           start=(bi == 0 and first), stop=(bi == nb - 1), skip_group_check=True)
                        return ins
                    P.op("pe", mav, reads=[eb.dep, vd], writes=odeps)
                    if bi == nb - 1 and t == 1:
                        epilogue(qi)

                def epilogue(qi):
                    q0, QW, nqb, oslot = qinfo[qi]
                    ssq = ssq_all[:, epi_n[0] % 8, :]
                    epi_n[0] += 1
                    nacc = 2 * nqb
                    QP = min(128, QW)

                    def cp(bix, bko, na):
                        P.op("dve", lambda e: e.tensor_copy(Oc[:QP, 3 * bix:3 * bix + na, :].rearrange("p a b -> p (a b)"), bko.ap[:QP, 0:129 * na]),
                             reads=[bko.dep], writes=[ocd])
                    for bix, bko in enumerate(obanks):
                        na = min(3, nacc - 3 * bix)
                        if na > 0:
                            cp(bix, bko, na)
                    oos = []

                    def blk1(j):
                        qw = min(128, QW - j * 128)
                        i1, i2 = j, nqb + j
                        tt = t1.next()
                        oo = o2.next()
                        P.op("dve", lambda e: e.reciprocal(rec[:qw, i1, 0:1], Oc[:qw, i1, 128:129]), reads=[ocd], writes=[ocd])
                        P.op("dve", lambda e: e.reciprocal(rec[:qw, i1, 1:2], Oc[:qw, i2, 128:129]), reads=[ocd], writes=[ocd])
                        P.op("dve", lambda e: e.tensor_tensor(rec[:qw, i1, 1:2], rec[:qw, i1, 1:2], neglam[:qw, :], ALU.mult), reads=[ocd, cdep], writes=[ocd])
                        P.op("dve", lambda e: e.tensor_scalar(tt.ap[:qw, :], Oc[:qw, i1, 0:128], rec[:qw, i1, 0:1], None, ALU.mult), reads=[ocd], writes=[tt.dep])
                        P.op("dve", lambda e: e.scalar_tensor_tensor(oo.ap[:qw, :], Oc[:qw, i2, 0:128], rec[:qw, i1, 1:2], tt.ap[:qw, :], ALU.mult, ALU.add),
                             reads=[ocd, tt.dep], writes=[oo.dep])
                        P.op("dve", lambda e: e.tensor_tensor(junk[:qw, :], oo.ap[:qw, :], oo.ap[:qw, :], ALU.mult), reads=[oo.dep], writes=[ocd])
                        P.op("dve", lambda e: e.reduce_sum(ssq[:qw, j:j + 1], junk[:qw, :], axis=mybir.AxisListType.X), reads=[ocd], writes=[ssqd])
                        oos.append(oo)
                    for j in range(nqb):
                        blk1(j)

                    def stage2():
                        P.op("act", lambda e: e.activation(ssq[:QP, 0:nqb], ssq[:QP, 0:nqb], AF.Ln, bias=epsb[:QP, :], scale=1.0 / 128.0), reads=[ssqd, cdep], writes=[ssqd])
                        P.op("act", lambda e: e.activation(ssq[:QP, 0:nqb], ssq[:QP, 0:nqb], AF.Exp, scale=-0.5), reads=[ssqd], writes=[ssqd])
                    abs_ = []

                    def stage3a():
                        def one(j):
                            qw = min(128, QW - j * 128)
                            ab = attb.next()
                            oo = oos[j]
                            P.op("dve", lambda e: e.scalar_tensor_tensor(ab.ap[:qw, :], oo.ap[:qw, :], ssq[:qw, j:j + 1], subw[:qw, :], ALU.mult, ALU.mult),
                                 reads=[ssqd, oo.dep, cdep], writes=[ab.dep])
                            abs_.append(ab)
                        for j in range(nqb):
                            one(j)

                    def stage3b():
                        def mtr(e):
                            for j in range(nqb):
                                qw = min(128, QW - j * 128)
                                ins = e.transpose(tp_bf[:, j * 128:j * 128 + qw], abs_[j].ap[:qw, :], ident[:qw, :qw])
                            return ins
                        P.op("pe", mtr, reads=[a_.dep for a_ in abs_] + [cdep], writes=[tbank.dep])

                    def stage4():
                        P.op("dve", lambda e: e.tensor_copy(MIX[:, h, q0:q0 + QW], tp_bf[:, 0:QW]), reads=[tbank.dep], writes=[mix_dep[h]])
                    defer(4 * EPI_DELAY, stage2)
                    defer(8 * EPI_DELAY, stage3a)
                    defer(12 * EPI_DELAY, stage3b)
                    defer(16 * EPI_DELAY, stage4)

                ebs = {}
                nxt = 0
                for i in range(len(units)):
                    while nxt < min(i + LA + 1, len(units)):
                        ebs[nxt] = front(units[nxt])
                        nxt += 1
                    back(units[i], ebs.pop(i))
                    gstate["gi"] += 1
                    run_pending()

            wb = load_w(wring, "watt", 0)
            for h in range(8):
                nwb = load_w(wring, "watt", h + 1) if h + 1 < 8 else None
                wdb = wdt_ring.next()
                zflat = zscr[kind][h]
                wd_src = bass.AP(tensor=zflat.tensor, offset=zflat.offset + 128, ap=[[ZW - 1, 128], [1, 1152]])
                P.dma("sp", lambda e, wdb=wdb, wd_src=wd_src: e.dma_start(out=wdb.ap, in_=wd_src), wdb.dsem,
                      reads=[zdep[kind][h]], writes=[wdb.dep])
                for (t0, T) in qtiles:
                    bk = bank(spool)

                    def mq(e, bk=bk, wb=wb, t0=t0, T=T):
                        for ch in range(8):
                            ins = e.matmul(bk.ap[:, :T], wb.ap[:, ch, 0:128], U[:, ch, t0:t0 + T], start=(ch == 0), stop=(ch == 7))
                        return ins
                    P.op("pe", mq, reads=[wb.dep] + U_all, writes=[bk.dep])
                    P.op("act", lambda e, bk=bk, t0=t0, T=T: e.copy(qz[0:64, 0, t0:t0 + T], bk.ap[0:64, :T]), reads=[bk.dep], writes=[qd])
                    P.op("act", lambda e, bk=bk, t0=t0, T=T: e.copy(qz[64:128, 1, t0:t0 + T], bk.ap[64:128, :T]), reads=[bk.dep], writes=[qd])
                for (t0, T) in ktiles:
                    bk = bank(spool)

                    def mk(e, bk=bk, wb=wb, t0=t0, T=T):
                        for ch in range(8):
                            ins = e.matmul(bk.ap[:, :T], wb.ap[:, ch, 128:256], U[:, ch, t0:t0 + T], start=(ch == 0), stop=(ch == 7))
                        return ins
                    P.op("pe", mk, reads=[wb.dep] + U_all, writes=[bk.dep])
                    P.op("dve", lambda e, bk=bk, t0=t0, T=T: e.tensor_copy(kT[:, t0:t0 + T], bk.ap[:, :T]), reads=[bk.dep], writes=[kd])
                for bi, (b0, KW) in enumerate(blks):
                    bk = bank(spool)

                    def mv(e, bk=bk, wb=wb, b0=b0, KW=KW):
                        for ch in range(8):
                            ins = e.matmul(bk.ap[:KW, 0:128], U[:, ch, b0:b0 + KW], wb.ap[:, ch, 256:384], start=(ch == 0), stop=(ch == 7))
                        return ins
                    P.op("pe", mv, reads=[wb.dep] + U_all, writes=[bk.dep])
                    eng = "act" if bi % 2 == 0 else "dve"
                    if eng == "act":
                        P.op("act", lambda e, bk=bk, bi=bi, KW=KW: e.copy(V1[:KW, bi, 0:128], bk.ap[:KW, 0:128]), reads=[bk.dep], writes=[vd])
                    else:
                        P.op("dve", lambda e, bk=bk, bi=bi, KW=KW: e.tensor_copy(V1[:KW, bi, 0:128], bk.ap[:KW, 0:128]), reads=[bk.dep], writes=[vd])
                attend_head(h, wdb)
                if nwb is not None:
                    wb = nwb
            run_pending(force=True)

        def ssm_phase(kind, L, U, U_all, MIX, mix_dep):
            wk = Bump(arena, [(R0, 8 * LO * 2), (R3, R3SZ)])
            blks = tiles_of(L, 128)
            nb = len(blks)
            nbo = len(tiles_of(LO, 128))
            vk = vec[kind]
            dtt = wk.alloc([nb, 32], F32)
            aa = wk.alloc([nb, 32], F32)
            ncum = wk.alloc([nb, 32], F32)
            wgt = wk.alloc([nb, 32], F32)
            etot = wk.alloc([nb, 32], F32)
            LOB = min(L, (LO + 127) // 128 * 128)
            BT = wk.alloc([LOB], BF16)
            CT = wk.alloc([LOB], BF16)
            Btok = wk.alloc([nb, 128], BF16)
            xT = wk.alloc([L], BF16)
            xtok = wk.alloc([nb, 128], BF16)
            zs = wk.alloc([LO], BF16)
            Hbin = wk.alloc([nbo, 2, 64], BF16)
            Hst = wk.alloc([2, 2, 64], F32)
            wdtb = Buf(wk.alloc([8, 32], BF16), P.dep(), P.dsem("wdt"))
            wzring = Ring([Buf(wk.alloc([8, 256], BF16), P.dep(), P.dsem("wzx%d" % i)) for i in range(2)])
            stg = Ring([Buf(wk.alloc([520], BF16), P.dep()) for _ in range(2)])
            dgr = Ring([Buf(wk.alloc([7, 128], BF16), P.dep()) for _ in range(2)])
            abr = Ring([Buf(wk.alloc([4, 128], F32), P.dep()) for _ in range(2)])
            dmr = Ring([Buf(wk.alloc([4, 128], F32), P.dep()) for _ in range(2)])
            ebr = Ring([Buf(wk.alloc([4, 128], F32), P.dep()) for _ in range(2)])
            wtr = Ring([Buf(wk.alloc([4, 128], BF16), P.dep()) for _ in range(3)])
            cer = Ring([Buf(wk.alloc([4, 128], BF16), P.dep()) for _ in range(3)])
            bwr = Ring([Buf(wk.alloc([2, 64], BF16), P.dep()) for _ in range(3)])
            gmr = Ring([Buf(wk.alloc([2, 128], BF16), P.dep()) for _ in range(2)])
            yvr = Ring([Buf(wk.alloc([128], F32), P.dep()) for _ in range(2)])
            hbr = Ring([Buf(wk.alloc([2, 64], BF16), P.dep()) for _ in range(3)])
            dtd, bcd, xd, zd, hd_, hbd = P.dep(), P.dep(), P.dep(), P.dep(), P.dep(), P.dep()
            btd = P.dep()
            pl = [0, 1, 2]
            cring = Ring([banks[0], banks[1]])
            gring = Ring([banks[2]])
            yring = Ring([banks[4], banks[5]])
            string = Ring([banks[6], banks[7]])
            tpring = Ring([banks[3], banks[4]])
            wname = "wdt" + kind

            wbcb = load_w(wzring, "wbc", 0)
            P.dma("sp", lambda e: e.dma_start(out=wdtb.ap, in_=wbf[wname][0].rearrange("p (a b) -> p a b", a=8)), wdtb.dsem,
                  reads=[wdep[wname][0]], writes=[wdtb.dep])

            for g0 in range(0, nb, 16):
                bk = bank(pl)
                gn = min(16, nb - g0)

                def mdt(e, bk=bk, g0=g0, gn=gn):
                    for i in range(gn):
                        b0, KW = blks[g0 + i]
                        for ch in range(8):
                            ins = e.matmul(bk.ap[:KW, i * 32:(i + 1) * 32], U[:, ch, b0:b0 + KW], wdtb.ap[:, ch, :], start=(ch == 0), stop=(ch == 7))
                    return ins
                P.op("pe", mdt, reads=[wdtb.dep] + U_all, writes=[bk.dep])
                for i in range(gn):
                    b0, KW = blks[g0 + i]
                    P.op("dve", lambda e, bk=bk, i=i, g0=g0, KW=KW: e.tensor_tensor(dtt[:KW, g0 + i, :], bk.ap[:KW, i * 32:(i + 1) * 32],
                                                                                     row[kind][:KW, 0:32], ALU.add),
                         reads=[bk.dep, cdep], writes=[dtd])
            nfull = nb - 1
            lastKW = blks[-1][1]

            fl = lambda a: a[:, 0:nfull, :]
            ll = lambda a: a[:lastKW, nfull:nb, :]
            for sel in (fl, ll):
                P.op("act", lambda e, sel=sel: e.activation(sel(dtt), sel(dtt), AF.Exp), reads=[dtd], writes=[dtd])
                P.op("act", lambda e, sel=sel: e.activation(sel(dtt), sel(dtt), AF.Ln, bias=1.0), reads=[dtd], writes=[dtd])
            P.op("dve", lambda e: e.tensor_tensor(fl(aa), fl(dtt), negA[kind].unsqueeze(1).to_broadcast([128, nfull, 32]), ALU.mult),
                 reads=[dtd, cdep], writes=[dtd])
            P.op("dve", lambda e: e.tensor_tensor(ll(aa), ll(dtt), negA[kind][:lastKW, :].unsqueeze(1).to_broadcast([lastKW, 1, 32]), ALU.mult),
                 reads=[dtd, cdep], writes=[dtd])
            for g0 in range(0, nb, 8):
                bk = bank(pl)
                bk2 = bank(pl)
                gn = min(8, nb - g0)

                def mcum(e, bk=bk, bk2=bk2, g0=g0, gn=gn):
                    for i in range(gn):
                        b0, KW = blks[g0 + i]
                        e.matmul(bk.ap[:KW, i * 32:i * 32 + 16], trif[:KW, :KW], aa[:KW, g0 + i, 0:16], start=True, stop=True)
                        e.matmul(bk.ap[:KW, i * 32 + 16:i * 32 + 32], trib[:KW, :KW], aa[:KW, g0 + i, 16:32], start=True, stop=True)
                        ins = e.matmul(bk2.ap[:, i * 32:i * 32 + 32], onesf[:KW, :], aa[:KW, g0 + i, :], start=True, stop=True)
                    return ins
                P.op("pe", mcum, reads=[dtd, cdep], writes=[bk.dep, bk2.dep])
                for i in range(gn):
                    b0, KW = blks[g0 + i]
                    P.op("dve", lambda e, bk=bk, bk2=bk2, i=i, g0=g0, KW=KW: e.tensor_copy(etot[:, g0 + i, :], bk2.ap[:, i * 32:i * 32 + 32]),
                         reads=[bk2.dep], writes=[dtd])
                    P.op("dve", lambda e, bk=bk, i=i, g0=g0, KW=KW: e.tensor_scalar(ncum[:KW, g0 + i, :], bk.ap[:KW, i * 32:i * 32 + 32], -1.0, None, ALU.mult),
                         reads=[bk.dep], writes=[dtd])
                    P.op("dve", lambda e, i=i, g0=g0, KW=KW: e.tensor_tensor(wgt[:KW, g0 + i, :], etot[:KW, g0 + i, :], ncum[:KW, g0 + i, :], ALU.add),
                         reads=[dtd], writes=[dtd])
            for sel in (fl, ll):
                P.op("act", lambda e, sel=sel: e.activation(sel(wgt), sel(wgt), AF.Exp), reads=[dtd], writes=[dtd])
                P.op("dve", lambda e, sel=sel: e.tensor_tensor(sel(wgt), sel(wgt), sel(dtt), ALU.mult), reads=[dtd], writes=[dtd])
            for sel in (fl, ll):
                P.op("act", lambda e, sel=sel: e.activation(sel(dtt), sel(dtt), AF.Ln), reads=[dtd], writes=[dtd])
                P.op("dve", lambda e, sel=sel: e.tensor_tensor(sel(ncum), sel(ncum), sel(dtt), ALU.add), reads=[dtd], writes=[dtd])
            P.op("act", lambda e: e.activation(etot, etot, AF.Exp), reads=[dtd], writes=[dtd])

            ctl = tiles_of(L, 506)

            def conv_chunk(wbuf, col0, cc, emit_out):
                dg = dgr.next()

                def mk_dg(k):
                    P.op("dve", lambda e: e.tensor_scalar(dg.ap[:, k, :], tmpf, vk[:, cc * 7 + k:cc * 7 + k + 1], None, ALU.mult), reads=[cdep], writes=[dg.dep])
                for k in range(7):
                    mk_dg(k)

                def tile(t0, T):
                    lo = max(t0 - 3, 0)
                    hi = min(t0 + T + 3, L)
                    n = hi - lo
                    bk = bank(pl)

                    def mp(e):
                        for ch in range(8):
                            ins = e.matmul(bk.ap[:, :n], wbuf.ap[:, ch, col0:col0 + 128], U[:, ch, lo:lo + n], start=(ch == 0), stop=(ch == 7))
                        return ins
                    P.op("pe", mp, reads=[wbuf.dep] + U_all, writes=[bk.dep])
                    sg = stg.next()
                    off = lo - (t0 - 3)
                    if off > 0:
                        P.op("pool", lambda e: e.memset(sg.ap[:, 0:off], 0.0), writes=[sg.dep])
                    if off + n < T + 6:
                        P.op("pool", lambda e: e.memset(sg.ap[:, off + n:T + 6], 0.0), writes=[sg.dep])
                    P.op("act", lambda e: e.copy(sg.ap[:, off:off + n], bk.ap[:, :n]), reads=[bk.dep], writes=[sg.dep])
                    co = bank(pl)

                    def mc(e):
                        for k in range(7):
                            ins = e.matmul(co.ap[:, :T], dg.ap[:, k, :], sg.ap[:, k:k + T], start=(k == 0), stop=(k == 6))
                        return ins
                    P.op("pe", mc, reads=[sg.dep, dg.dep], writes=[co.dep])
                    emit_out(co, t0, T)
                for (t0, T) in ctl:
                    tile(t0, T)

            def transpose_all(src, sdep, dst, ddep):
                def group(bis):
                    tb = tpring.next()
                    tpv = tb.ap[:, 0:256].bitcast(BF16)
                    kw = blks[bis[0]][1]

                    def mt(e):
                        for i, bi in enumerate(bis):
                            b0, KW = blks[bi]
                            ins = e.transpose(tpv[:KW, i * 128:(i + 1) * 128], src[:, b0:b0 + KW], ident)
                        return ins
                    P.op("pe", mt, reads=[sdep, cdep], writes=[tb.dep])
                    ng = len(bis)
                    P.op("dve", lambda e: e.tensor_copy(dst[:kw, bis[0]:bis[0] + ng, :], tpv[:kw, 0:128 * ng].rearrange("p (a b) -> p a b", a=ng)),
                         reads=[tb.dep], writes=[ddep])
                full = [bi for bi in range(nb) if blks[bi][1] == 128]
                for i in range(0, len(full), 4):
                    group(full[i:i + 4])
                for bi in range(nb):
                    if blks[bi][1] != 128:
                        group([bi])

            def out_B(co, t0, T):
                P.op("act", lambda e: e.activation(xT[:, t0:t0 + T], co.ap[:, :T], AF.Silu, bias=vk[:, 78:79]), reads=[co.dep, cdep], writes=[xd])
            conv_chunk(wbcb, 0, 8, out_B)
            transpose_all(xT, xd, Btok, btd)
            P.op("dve", lambda e: e.tensor_copy(BT, xT[:, 0:LOB]), reads=[xd], writes=[bcd])

            def out_C(co, t0, T):
                T2 = min(T, LOB - t0)
                if T2 > 0:
                    P.op("act", lambda e: e.activation(CT[:, t0:t0 + T2], co.ap[:, :T2], AF.Silu, bias=vk[:, 79:80]), reads=[co.dep, cdep], writes=[bcd])
            conv_chunk(wbcb, 128, 9, out_C)

            def do_pair(j, wz):
                g = j // 4
                gs = slice(g * 64, g * 64 + 64)

                def out_x(co, t0, T):
                    P.op("act", lambda e: e.activation(xT[:, t0:t0 + T], co.ap[:, :T], AF.Silu, bias=vk[:, 70 + j:71 + j]), reads=[co.dep, cdep], writes=[xd])
                conv_chunk(wz, 128, j, out_x)
                transpose_all(xT, xd, xtok, xd)

                def do_z(t0, T):
                    bk = bank(pl)

                    def mz(e):
                        for ch in range(8):
                            ins = e.matmul(bk.ap[:, :T], wz.ap[:, ch, 0:128], U[:, ch, t0:t0 + T], start=(ch == 0), stop=(ch == 7))
                        return ins
                    P.op("pe", mz, reads=[wz.dep] + U_all, writes=[bk.dep])
                    P.op("act", lambda e: e.activation(zs[:, t0:t0 + T], bk.ap[:, :T], AF.Silu), reads=[bk.dep], writes=[zd])
                for (t0, T) in tiles_of(LO):
                    do_z(t0, T)

                P.op("pool", lambda e: e.memset(Hst, 0.0), writes=[hd_])
                hb0 = hbr.next()
                P.op("pool", lambda e: e.memset(hb0.ap, 0.0), writes=[hb0.dep])

                def st_front(bi, dirn):
                    b0, KW = blks[bi]
                    c0 = dirn * 16 + 2 * j
                    bw = bwr.next()
                    P.op("dve", lambda e: e.tensor_tensor(bw.ap[:KW, :, :], Btok[:KW, bi, g * 64:g * 64 + 64].unsqueeze(1).to_broadcast([KW, 2, 64]),
                                                          wgt[:KW, bi, c0:c0 + 2].unsqueeze(2).to_broadcast([KW, 2, 64]), ALU.mult),
                         reads=[btd, dtd], writes=[bw.dep])
                    sb_ = string.next()

                    def mst(e):
                        e.matmul(sb_.ap[gs, 0:64], bw.ap[:KW, 0, :], xtok[:KW, bi, 0:64], start=True, stop=True)
                        return e.matmul(sb_.ap[gs, 64:128], bw.ap[:KW, 1, :], xtok[:KW, bi, 64:128], start=True, stop=True)
                    P.op("pe", mst, reads=[bw.dep, xd], writes=[sb_.dep])
                    return sb_

                def st_update(bi, dirn, sb_):
                    c0 = dirn * 16 + 2 * j

                    def one(e_):
                        P.op("dve", lambda e: e.scalar_tensor_tensor(Hst[gs, dirn, e_, :], Hst[gs, dirn, e_, :], etot[gs, bi, c0 + e_:c0 + e_ + 1],
                                                                     sb_.ap[gs, e_ * 64:e_ * 64 + 64], ALU.mult, ALU.add),
                             reads=[sb_.dep, hd_, dtd], writes=[hd_])
                    one(0)
                    one(1)

                def save_hbin(bi):
                    P.op("dve", lambda e: e.tensor_copy(Hbin[gs, bi, :, :], Hst[gs, 1, :, :]), reads=[hd_], writes=[hbd])

                border = list(range(nb - 1, -1, -1))
                bsb = {}
                bbw = {}

                def b1(bi):
                    b0, KW = blks[bi]
                    c0 = 16 + 2 * j
                    bw = bwr.next()
                    P.op("dve", lambda e: e.tensor_tensor(bw.ap[:KW, :, :], Btok[:KW, bi, g * 64:g * 64 + 64].unsqueeze(1).to_broadcast([KW, 2, 64]),
                                                          wgt[:KW, bi, c0:c0 + 2].unsqueeze(2).to_broadcast([KW, 2, 64]), ALU.mult),
                         reads=[btd, dtd], writes=[bw.dep])
                    bbw[bi] = bw

                def st_mm(bi, bw):
                    b0, KW = blks[bi]
                    sb_ = string.next()

                    def mst(e):
                        e.matmul(sb_.ap[gs, 0:64], bw.ap[:KW, 0, :], xtok[:KW, bi, 0:64], start=True, stop=True)
                        return e.matmul(sb_.ap[gs, 64:128], bw.ap[:KW, 1, :], xtok[:KW, bi, 64:128], start=True, stop=True)
                    P.op("pe", mst, reads=[bw.dep, xd], writes=[sb_.dep])
                    return sb_

                def b2(bi):
                    bsb[bi] = st_mm(bi, bbw.pop(bi))

                def b3(bi):
                    if bi < nbo:
                        save_hbin(bi)
                    if bi > 0:
                        st_update(bi, 1, bsb.pop(bi))
                for k in range(-2, len(border)):
                    if 0 <= k < len(border):
                        b3(border[k])
                    if 0 <= k + 1 < len(border) and border[k + 1] > 0:
                        b2(border[k + 1])
                    if 0 <= k + 2 < len(border) and border[k + 2] > 0:
                        b1(border[k + 2])

                cx = {}

                def s1(c):
                    b0, KW = blks[c]
                    d = cx[c] = {}
                    abc = d["abc"] = abr.next()

                    def mk_abc(dirn):
                        c0 = dirn * 16 + 2 * j
                        P.op("dve", lambda e: e.tensor_copy(abc.ap[:KW, 2 * dirn:2 * dirn + 2, :], aa[:KW, c, c0:c0 + 2].unsqueeze(2).to_broadcast([KW, 2, 128])),
                             reads=[dtd], writes=[abc.dep])
                    mk_abc(0)
                    mk_abc(1)
                    if c + 1 < nbo:
                        bw = d["bw"] = bwr.next()
                        c0 = 2 * j
                        P.op("dve", lambda e: e.tensor_tensor(bw.ap[:KW, :, :], Btok[:KW, c, g * 64:g * 64 + 64].unsqueeze(1).to_broadcast([KW, 2, 64]),
                                                              wgt[:KW, c, c0:c0 + 2].unsqueeze(2).to_broadcast([KW, 2, 64]), ALU.mult),
                             reads=[btd, dtd], writes=[bw.dep])

                def s2(c):
                    b0, KW = blks[c]
                    d = cx[c]
                    gbk = d["gb"] = gring.next()
                    P.op("pe", lambda e: e.matmul(gbk.ap[:KW, :KW], BT[gs, b0:b0 + KW], CT[gs, b0:b0 + KW], start=True, stop=True),
                         reads=[bcd], writes=[gbk.dep])
                    cb_ = d["cb"] = cring.next()
                    abc = d["abc"]

                    def mcb(e):
                        for u in range(4):
                            tri = trif if u < 2 else trib
                            ins = e.matmul(cb_.ap[:, u * 128:u * 128 + KW], abc.ap[:KW, u, :], tri[:KW, :KW], start=True, stop=True)
                        return ins
                    P.op("pe", mcb, reads=[abc.dep, cdep], writes=[cb_.dep])
                    if "bw" in d:
                        d["sb"] = st_mm(c, d["bw"])

                def s3(c):
                    b0, KW = blks[c]
                    d = cx[c]
                    gm = d["gm"] = gmr.next()
                    gbk = d["gb"]
                    P.op("dve", lambda e: e.tensor_tensor(gm.ap[:KW, :, :KW], gbk.ap[:KW, :KW].unsqueeze(1).to_broadcast([KW, 2, KW]), tri2[:KW, :, :KW], ALU.mult),
                         reads=[gbk.dep, cdep], writes=[gm.dep])
                    dm = d["dm"] = dmr.next()
                    cb_ = d["cb"]

                    def mk_dm(u):
                        hcol = (u // 2) * 16 + 2 * j + (u % 2)
                        P.op("act", lambda e: e.activation(dm.ap[:KW, u, :KW], cb_.ap[:KW, u * 128:u * 128 + KW], AF.Exp, bias=ncum[:KW, c, hcol:hcol + 1]),
                             reads=[cb_.dep, dtd], writes=[dm.dep])
                    for u in range(4):
                        mk_dm(u)
                    eb = d["eb"] = ebr.next()
                    P.op("act", lambda e: e.activation(eb.ap[:, :, :KW], cb_.ap.rearrange("p (u k) -> p u k", u=4)[:, :, :KW], AF.Exp), reads=[cb_.dep], writes=[eb.dep])

                def s4(c):
                    b0, KW = blks[c]
                    d = cx[c]
                    wt = d["wt"] = wtr.next()
                    dm, gm, eb = d["dm"], d["gm"], d["eb"]

                    def mk_wt(dirn):
                        P.op("dve", lambda e: e.scalar_tensor_tensor(wt.ap[:KW, 2 * dirn:2 * dirn + 2, :KW], dm.ap[:KW, 2 * dirn:2 * dirn + 2, :KW], 1e30,
                                                                     gm.ap[:KW, dirn, :KW].unsqueeze(1).to_broadcast([KW, 2, KW]), ALU.min, ALU.mult),
                             reads=[dm.dep, gm.dep], writes=[wt.dep])
                    mk_wt(0)
                    mk_wt(1)
                    ce = d["ce"] = cer.next()
                    P.op("dve", lambda e: e.tensor_tensor(ce.ap[gs, :, :KW], eb.ap[gs, :, :KW], CT[gs, b0:b0 + KW].unsqueeze(1).to_broadcast([64, 4, KW]), ALU.mult),
                         reads=[eb.dep, bcd], writes=[ce.dep])

                def s5(c):
                    d = cx[c]
                    if c + 1 < nbo:
                        st_update(c, 0, d["sb"])
                        hb2 = hbr.next()
                        P.op("dve", lambda e: e.tensor_copy(hb2.ap[gs, :, :], Hst[gs, 0, :, :]), reads=[hd_], writes=[hb2.dep])
                        cx.setdefault(c + 1, {})
                        hbs[c + 1] = hb2

                def s6(c):
                    b0, KW = blks[c]
                    d = cx[c]
                    wt, ce, hb = d["wt"], d["ce"], hbs[c]
                    yb = d["yb"] = yring.next()

                    def my(e):
                        for e_ in range(2):
                            o = yb.ap[e_ * 64:e_ * 64 + 64, :KW]
                            xs_ = xtok[:KW, c, e_ * 64:e_ * 64 + 64]
                            e.matmul(o, xs_, wt.ap[:KW, e_, :KW], start=True, stop=False)
                            e.matmul(o, xs_, wt.ap[:KW, 2 + e_, :KW], start=False, stop=False)
                            e.matmul(o, hb.ap[gs, e_, :], ce.ap[gs, e_, :KW], start=False, stop=False)
                            ins = e.matmul(o, Hbin[gs, c, e_, :], ce.ap[gs, 2 + e_, :KW], start=False, stop=True)
                        return ins
                    P.op("pe", my, reads=[xd, hb.dep, hbd, wt.dep, ce.dep], writes=[yb.dep])

                def s7(c):
                    b0, KW = blks[c]
                    yb = cx[c]["yb"]
                    yv = yvr.next()
                    P.op("dve", lambda e: e.scalar_tensor_tensor(yv.ap[:, :KW], xT[:, b0:b0 + KW], vk[:, 80 + j:81 + j], yb.ap[:, :KW], ALU.mult, ALU.add),
                         reads=[yb.dep, xd, cdep], writes=[yv.dep])
                    KO = min(KW, LO - b0)
                    P.op("dve", lambda e: e.tensor_tensor(MIX[:, 8 + j, b0:b0 + KO], yv.ap[:, :KO], zs[:, b0:b0 + KO], ALU.mult),
                         reads=[yv.dep, zd], writes=[mix_dep[8 + j]])
                    del cx[c]

                hbs = {0: hb0}
                stages = [(s7, -2), (s6, -1), (s5, 0), (s4, 0), (s3, 1), (s2, 2), (s1, 3)]
                for i in range(-3, nbo + 2):
                    for fn, off in stages:
                        c = i + off
                        if 0 <= c < nbo:
                            fn(c)

            wz = load_w(wzring, "wzx", 0)
            for j in range(8):
                nwz = load_w(wzring, "wzx", j + 1) if j + 1 < 8 else None
                do_pair(j, wz)
                if nwz is not None:
                    wz = nwz

        out_dep = P.dep()
        if do_S:
            run_seq("S", xs, LS, ys)
        if do_P:
            run_seq("P", xp, LP, yp)
        P.op("sp", None, reads=[out_dep])
        P.op("pool", None, reads=[out_dep])
        P.emit(nc, st)
    return nc


def _oh_const():
    m = np.arange(ZW)
    b = t5_bucket_np(640 - m)
    oh = np.zeros((32, ZW), np.float32)
    oh[b, m] = 1.0
    return oh


def _pc(v):
    v = np.asarray(v, np.float32).reshape(-1, 128)
    return np.ascontiguousarray(v.T)


def prep_inputs(inp):
    f = lambda a: np.ascontiguousarray(np.asarray(a, np.float32))
    w_in = f(inp["w_in"])[0]
    common = {}

    def gu(w):
        return np.ascontiguousarray(w.reshape(8, 128, 11, 256).transpose(2, 1, 0, 3).reshape(11, 128, 2048))

    def dn(w):
        return np.ascontiguousarray(w.reshape(22, 128, 8, 128).transpose(2, 1, 0, 3).reshape(8, 128, 2816))

    def cols(w, c0, n):
        return np.ascontiguousarray(w[:, c0:c0 + n].reshape(8, 128, n).transpose(1, 0, 2).reshape(128, 8 * n))
    common["wg1"] = gu(f(inp["ffn1_w_gate"])[0])
    common["wu1"] = gu(f(inp["ffn1_w_up"])[0])
    common["wd1"] = dn(f(inp["ffn1_w_down"])[0])
    common["wg2"] = gu(f(inp["ffn2_w_gate"])[0])
    common["wu2"] = gu(f(inp["ffn2_w_up"])[0])
    common["wd2"] = dn(f(inp["ffn2_w_down"])[0])
    watt = np.zeros((8, 128, 8, 384), np.float32)
    for h in range(8):
        for s, base in enumerate((0, 1024, 2048)):
            watt[h, :, :, s * 128:(s + 1) * 128] = w_in[:, base + h * 128:base + (h + 1) * 128].reshape(8, 128, 128).transpose(1, 0, 2)
    common["watt"] = watt.reshape(8, 128, 3072)
    wzx = np.zeros((8, 128, 8, 256), np.float32)
    for j in range(8):
        wzx[j, :, :, 0:128] = w_in[:, 3072 + j * 128:3072 + (j + 1) * 128].reshape(8, 128, 128).transpose(1, 0, 2)
        wzx[j, :, :, 128:256] = w_in[:, 4096 + j * 128:4096 + (j + 1) * 128].reshape(8, 128, 128).transpose(1, 0, 2)
    common["wzx"] = wzx.reshape(8, 128, 2048)
    common["wbc"] = cols(w_in, 5120, 256)[None]
    wdt_nat = cols(w_in, 5376, 32)[None]
    w_dt_sw = np.concatenate([w_in[:, 5392:5408], w_in[:, 5376:5392]], axis=1)
    wdt_swp = cols(w_dt_sw, 0, 32)[None]
    wo = f(inp["w_out"])[0]
    common["wo"] = np.ascontiguousarray(wo.reshape(16, 128, 8, 128).transpose(2, 1, 0, 3).reshape(8, 128, 2048))
    common["nrm"] = np.concatenate([_pc(inp["ffn1_norm_w"][0]), _pc(inp["mix_norm_w"][0]), _pc(inp["ffn2_norm_w"][0]),
                                    _pc(inp["final_norm_w"]), _pc(inp["ssm_norm_w"][0])], axis=1)
    common["lam"] = np.concatenate([f(inp["lambda_q1"])[0], f(inp["lambda_k1"])[0], f(inp["lambda_q2"])[0], f(inp["lambda_k2"])[0]])[None]
    common["oh"] = _oh_const()
    conv_w = f(inp["conv_w"])[0]
    conv_b = f(inp["conv_b"])[0]
    dsk = np.repeat(f(inp["ssm_d"])[0], 64)

    def vecs(cw):
        v = np.zeros((128, 88), np.float32)
        v[:, 0:70] = cw.T.reshape(10, 128, 7).transpose(1, 0, 2).reshape(128, 70)
        v[:, 70:80] = _pc(conv_b)
        v[:, 80:88] = _pc(dsk)
        return v

    def rows(dbf, dbb, alf, alb):
        r = np.zeros((1, 192), np.float32)
        r[0, 0:16] = dbf
        r[0, 16:32] = dbb
        r[0, 32:48] = alf
        r[0, 48:64] = alb
        r[0, 64:192] = f(inp["attn_subln_w"])[0]
        return r
    dbf, dbb = f(inp["dt_bias_fwd"])[0], f(inp["dt_bias_bwd"])[0]
    alf, alb = f(inp["a_log_fwd"])[0], f(inp["a_log_bwd"])[0]
    nat = {"vec": vecs(conv_w), "row": rows(dbf, dbb, alf, alb), "relb": f(inp["rel_bias"]), "wdt": wdt_nat}
    T = f(inp["rel_bias"])
    Tp = np.concatenate([T[0:1], T[17:32], T[16:17], T[1:16]], axis=0)
    rev = {"vec": vecs(conv_w[::-1]), "row": rows(dbb, dbf, alb, alf), "relb": np.ascontiguousarray(Tp), "wdt": wdt_swp}
    meta = f(inp["meta_tokens"])
    xsamp = f(inp["x_sample"])
    xprom = f(inp["x_prompt"])
    in_maps = []
    for c in range(NCORES):
        m = dict(common)
        hs = np.concatenate([meta, xsamp[c]], axis=0)
        m["xs"] = np.ascontiguousarray(hs.T)
        hp = np.concatenate([meta, xprom[c // 2]], axis=0)
        pk = nat
        if c % 2 == 1:
            hp = hp[::-1]
            pk = rev
        m["xp"] = np.ascontiguousarray(hp.T)
        m["vecS"], m["rowS"], m["relbS"], m["wdtS"] = nat["vec"], nat["row"], nat["relb"], nat["wdt"]
        m["vecP"], m["rowP"], m["relbP"], m["wdtP"] = pk["vec"], pk["row"], pk["relb"], pk["wdt"]
        in_maps.append(m)
    return in_maps


def assemble(results):
    nprom, nsamp = LP - NMETA, LS - NMETA
    half = nprom // 2
    y_prompt = np.zeros((4, nprom, 1024), np.float32)
    y_sample = np.zeros((8, nsamp, 1024), np.float32)
    for c in range(len(results)):
        r = results[c]
        y_sample[c] = r["ys"].T[NMETA:]
        ypT = r["yp"].T
        if c % 2 == 0:
            y_prompt[c // 2, 0:half] = ypT[NMETA:NMETA + half]
        else:
            y_prompt[c // 2, half:nprom] = ypT[0:half][::-1]
    return y_prompt, y_sample


_NC_CACHE = {}


def kernel(**inputs):
    in_maps = prep_inputs(inputs)
    if "nc" not in _NC_CACHE:
        _NC_CACHE["nc"] = build_program()
    nc = _NC_CACHE["nc"]
    res = run_bass_kernel_spmd(nc, in_maps, core_ids=list(range(8)))
    return assemble(res.results)
```
